# Optimizing a Trainium2 kernel written in Bass

```python
import jax, jax.numpy as jnp
from jax import lax
import numpy as np

D_MODEL = 4096
BATCH = 16
SEQ = 256
DEPTH = 1
DEC_BATCH = 2
DEC_SEQ = 2048
PAST_LEN = 256

GRID_W = 64
HEAD_DIM = 128
N_Q_HEADS = 16
N_KV_HEADS = 4
GQA_GROUP = N_Q_HEADS // N_KV_HEADS
ATTN_WIDTH = N_Q_HEADS * HEAD_DIM
KV_WIDTH = N_KV_HEADS * HEAD_DIM
CONV_WIDTH = D_MODEL - ATTN_WIDTH
CONV_K = 3
IN_WIDTH = ATTN_WIDTH + 2 * KV_WIDTH + 3 * CONV_WIDTH
WINDOW = 128
BLOCK = 128
D_FF = 11008
ROPE_THETA = 10000.0
ROPE_AXIS_DIM = HEAD_DIM // 2
ROPE_PAIRS = ROPE_AXIS_DIM // 2
N_SUB = 3
EPS = 1e-6
NEG = -1e30

kernel_name = "hybrid_swa_shortconv_macaron_dit_step"


def _rmsnorm(x, g):
    xf = x.astype(jnp.float32)
    y = xf * lax.rsqrt(jnp.mean(xf * xf, axis=-1, keepdims=True) + EPS)
    return (y * g.astype(jnp.float32)).astype(x.dtype)


def _swiglu(u, wg, wu, wd):
    return (jax.nn.silu(u @ wg) * (u @ wu)) @ wd


def _modulation(cond, w_mod_l, b_mod_l):
    m = jax.nn.silu(cond) @ w_mod_l + b_mod_l
    return m.reshape(m.shape[0], N_SUB, 3, D_MODEL)[:, :, :, None, :]


def _pre(h, m, s, g_pre):
    return _rmsnorm(h, g_pre) * (1 + m[:, s, 1]) + m[:, s, 0]


def _post(h, o, m, s, g_post, res_w):
    return h + res_w * m[:, s, 2] * _rmsnorm(o, g_post)


def _rotate_axis(xa, pos):
    inv_freq = ROPE_THETA ** (-jnp.arange(ROPE_PAIRS, dtype=jnp.float32) / ROPE_PAIRS)
    ang = pos.astype(jnp.float32)[:, None] * inv_freq[None, :]
    extra = xa.ndim - 3
    ang = ang.reshape(ang.shape[0], *([1] * extra), ROPE_PAIRS)
    cos, sin = jnp.cos(ang), jnp.sin(ang)
    xf = xa.astype(jnp.float32)
    x1, x2 = xf[..., :ROPE_PAIRS], xf[..., ROPE_PAIRS:]
    out = jnp.concatenate([x1 * cos - x2 * sin, x2 * cos + x1 * sin], axis=-1)
    return out.astype(xa.dtype)


def _rope_2d(x):
    T = x.shape[1]
    rows = T // GRID_W
    row_ids = jnp.repeat(jnp.arange(rows), GRID_W)
    col_ids = jnp.tile(jnp.arange(GRID_W), rows)
    return jnp.concatenate([_rotate_axis(x[..., :ROPE_AXIS_DIM], row_ids),
                            _rotate_axis(x[..., ROPE_AXIS_DIM:], col_ids)], axis=-1)


def _attend(q, k, v, mask, sink):
    s = jnp.einsum('bqhgd,bkhd->bhgqk', q, k, preferred_element_type=jnp.float32) * (HEAD_DIM ** -0.5)
    if mask is not None:
        s = jnp.where(mask, s, NEG)
    sink_col = jnp.broadcast_to(sink.astype(jnp.float32).reshape(1, N_KV_HEADS, GQA_GROUP, 1, 1),
                                s.shape[:-1] + (1,))
    p = jax.nn.softmax(jnp.concatenate([s, sink_col], axis=-1), axis=-1)[..., :-1]
    return jnp.einsum('bhgqk,bkhd->bqhgd', p.astype(v.dtype), v)


def _to_blocks(q):
    B, T = q.shape[:2]
    return q.reshape(B, T // BLOCK, BLOCK, N_KV_HEADS, GQA_GROUP, HEAD_DIM).transpose(1, 0, 2, 3, 4, 5)


def _from_blocks(ob):
    nb, B = ob.shape[:2]
    return ob.transpose(1, 0, 2, 3, 4, 5).reshape(B, nb * BLOCK, ATTN_WIDTH)


def _context_attention(q, k, v, sink):
    ob = lax.map(lambda qi: _attend(qi, k, v, None, sink), _to_blocks(q))
    return _from_blocks(ob)


def _latent_attention(q, k, v, k_ctx, v_ctx, sink):
    T = q.shape[1]
    nb = T // BLOCK
    P = k_ctx.shape[1]
    pad = ((0, 0), (BLOCK, BLOCK), (0, 0), (0, 0))
    k_pad, v_pad = jnp.pad(k, pad), jnp.pad(v, pad)
    q_off = jnp.arange(BLOCK)
    k_off = jnp.arange(3 * BLOCK) - BLOCK
    in_window = jnp.abs(k_off[None, :] - q_off[:, None]) <= WINDOW
    ctx_mask = jnp.ones((BLOCK, P), dtype=bool)

    def step(args):
        blk, qi = args
        kw = lax.dynamic_slice_in_dim(k_pad, blk * BLOCK, 3 * BLOCK, axis=1)
        vw = lax.dynamic_slice_in_dim(v_pad, blk * BLOCK, 3 * BLOCK, axis=1)
        key_pos = blk * BLOCK + k_off
        valid = in_window & ((key_pos >= 0) & (key_pos < T))[None, :]
        keys = jnp.concatenate([kw, k_ctx], axis=1)
        vals = jnp.concatenate([vw, v_ctx], axis=1)
        mask = jnp.concatenate([valid, ctx_mask], axis=1)
        return _attend(qi, keys, vals, mask, sink)

    ob = lax.map(step, (jnp.arange(nb), _to_blocks(q)))
    return _from_blocks(ob)


def _project(u, w_in_l):
    B, T = u.shape[:2]
    z = u @ w_in_l
    o1 = ATTN_WIDTH
    o2 = o1 + KV_WIDTH
    o3 = o2 + KV_WIDTH
    o4 = o3 + CONV_WIDTH
    o5 = o4 + CONV_WIDTH
    q = z[..., :o1].reshape(B, T, N_KV_HEADS, GQA_GROUP, HEAD_DIM)
    k = z[..., o1:o2].reshape(B, T, N_KV_HEADS, HEAD_DIM)
    v = z[..., o2:o3].reshape(B, T, N_KV_HEADS, HEAD_DIM)
    return q, k, v, z[..., o3:o4], z[..., o4:o5], z[..., o5:]


def _short_conv(gb, gc, hc, w_conv_l):
    u = gc * hc
    up = jnp.pad(u, ((0, 0), (1, 1), (0, 0)))
    y = up[:, :-2] * w_conv_l[0] + up[:, 1:-1] * w_conv_l[1] + up[:, 2:] * w_conv_l[2]
    return gb * y


def _merge(attn_o, conv_o, g_a, g_c, w_o_l):
    return jnp.concatenate([_rmsnorm(attn_o, g_a), _rmsnorm(conv_o, g_c)], axis=-1) @ w_o_l


def setup_inputs(seed: int = 0) -> dict:
    key = jax.random.key(seed)
    ks = jax.random.split(key, 24)
    f32 = jnp.float32
    nrm = lambda k, shape, s: jax.random.normal(k, shape, f32) * s
    return {
        "x_prompt": nrm(ks[0], (BATCH, SEQ, D_MODEL), 1.0),
        "x_sample": nrm(ks[1], (DEC_BATCH, DEC_SEQ, D_MODEL), 1.0),
        "c": nrm(ks[2], (DEC_BATCH, D_MODEL), 1.0),
        "cache_k": nrm(ks[3], (DEC_BATCH, DEPTH, PAST_LEN, N_KV_HEADS, HEAD_DIM), 1.0),
        "cache_v": nrm(ks[4], (DEC_BATCH, DEPTH, PAST_LEN, N_KV_HEADS, HEAD_DIM), 1.0),
        "c_ctx": nrm(ks[5], (D_MODEL,), 1.0),
        "w_mod": nrm(ks[6], (DEPTH, D_MODEL, N_SUB * 3 * D_MODEL), 0.5 * D_MODEL ** -0.5),
        "b_mod": nrm(ks[7], (DEPTH, N_SUB * 3 * D_MODEL), 0.02),
        "g_pre": 1.0 + nrm(ks[8], (DEPTH, N_SUB, D_MODEL), 0.05),
        "g_post": 1.0 + nrm(ks[9], (DEPTH, N_SUB, D_MODEL), 0.05),
        "w_in": nrm(ks[10], (DEPTH, D_MODEL, IN_WIDTH), D_MODEL ** -0.5),
        "w_conv": nrm(ks[11], (DEPTH, CONV_K, CONV_WIDTH), CONV_K ** -0.5),
        "sink": nrm(ks[12], (DEPTH, N_Q_HEADS), 0.5),
        "g_attn_out": 1.0 + nrm(ks[13], (DEPTH, ATTN_WIDTH), 0.05),
        "g_conv_out": 1.0 + nrm(ks[14], (DEPTH, CONV_WIDTH), 0.05),
        "w_o": nrm(ks[15], (DEPTH, D_MODEL, D_MODEL), D_MODEL ** -0.5),
        "w_ffn1_gate": nrm(ks[16], (DEPTH, D_MODEL, D_FF), D_MODEL ** -0.5),
        "w_ffn1_up": nrm(ks[17], (DEPTH, D_MODEL, D_FF), D_MODEL ** -0.5),
        "w_ffn1_down": nrm(ks[18], (DEPTH, D_FF, D_MODEL), D_FF ** -0.5),
        "w_ffn2_gate": nrm(ks[19], (DEPTH, D_MODEL, D_FF), D_MODEL ** -0.5),
        "w_ffn2_up": nrm(ks[20], (DEPTH, D_MODEL, D_FF), D_MODEL ** -0.5),
        "w_ffn2_down": nrm(ks[21], (DEPTH, D_FF, D_MODEL), D_FF ** -0.5),
    }


def reference(x_prompt, x_sample, c, cache_k, cache_v, c_ctx, w_mod, b_mod, g_pre, g_post,
              w_in, w_conv, sink, g_attn_out, g_conv_out, w_o,
              w_ffn1_gate, w_ffn1_up, w_ffn1_down, w_ffn2_gate, w_ffn2_up, w_ffn2_down):
    h = x_prompt
    ks_new, vs_new = [], []
    for l in range(DEPTH):
        m = _modulation(c_ctx[None, :], w_mod[l], b_mod[l])
        o = _swiglu(_pre(h, m, 0, g_pre[l, 0]), w_ffn1_gate[l], w_ffn1_up[l], w_ffn1_down[l])
        h = _post(h, o, m, 0, g_post[l, 0], 0.5)
        q, k, v, gb, gc, hc = _project(_pre(h, m, 1, g_pre[l, 1]), w_in[l])
        attn_o = _context_attention(q, k, v, sink[l])
        conv_o = _short_conv(gb, gc, hc, w_conv[l])
        o = _merge(attn_o, conv_o, g_attn_out[l], g_conv_out[l], w_o[l])
        h = _post(h, o, m, 1, g_post[l, 1], 1.0)
        o = _swiglu(_pre(h, m, 2, g_pre[l, 2]), w_ffn2_gate[l], w_ffn2_up[l], w_ffn2_down[l])
        h = _post(h, o, m, 2, g_post[l, 2], 0.5)
        ks_new.append(k)
        vs_new.append(v)
    y_prompt = h
    state_k = jnp.stack(ks_new, axis=1)
    state_v = jnp.stack(vs_new, axis=1)

    h = x_sample
    for l in range(DEPTH):
        m = _modulation(c, w_mod[l], b_mod[l])
        o = _swiglu(_pre(h, m, 0, g_pre[l, 0]), w_ffn1_gate[l], w_ffn1_up[l], w_ffn1_down[l])
        h = _post(h, o, m, 0, g_post[l, 0], 0.5)
        q, k, v, gb, gc, hc = _project(_pre(h, m, 1, g_pre[l, 1]), w_in[l])
        q, k = _rope_2d(q), _rope_2d(k)
        attn_o = _latent_attention(q, k, v, cache_k[:, l], cache_v[:, l], sink[l])
        conv_o = _short_conv(gb, gc, hc, w_conv[l])
        o = _merge(attn_o, conv_o, g_attn_out[l], g_conv_out[l], w_o[l])
        h = _post(h, o, m, 1, g_post[l, 1], 1.0)
        o = _swiglu(_pre(h, m, 2, g_pre[l, 2]), w_ffn2_gate[l], w_ffn2_up[l], w_ffn2_down[l])
        h = _post(h, o, m, 2, g_post[l, 2], 0.5)
    y_sample = h
    return (y_prompt, y_sample, state_k, state_v)
```

```python
import math
from contextlib import ExitStack

import numpy as np
import concourse.bass as bass
import concourse.mybir as mybir
from concourse.bass_utils import run_bass_kernel_spmd

F32 = mybir.dt.float32
BF16 = mybir.dt.bfloat16
AF = mybir.ActivationFunctionType
ALU = mybir.AluOpType
AX = mybir.AxisListType

FULL = dict(D=4096, FF=11008, HQ=16, KV=4)
EPS = 1e-6
NCORES = 8
SEQ = 256
OWN = 512
HALO = 128
PAST = 256
GRID_W = 64
DEC_SEQ = 2048
NSLOT = 4
MOD_UPFRONT_SEGS = 9
SLOT = 4096
ARENA_WORDS = 47 * 1024
import os as _os
SAME_ENGINE_INORDER = _os.environ.get('SEI', '0') == '1'


def derive(cfg):
    c = dict(cfg)
    c["G"] = c["HQ"] // c["KV"]
    c["AW"] = c["HQ"] * 128
    c["KW"] = c["KV"] * 128
    c["CW"] = c["D"] - c["AW"]
    c["IN"] = c["AW"] + 2 * c["KW"] + 3 * c["CW"]
    c["DC"] = c["D"] // 128
    c["FC"] = c["FF"] // 128
    c["CC"] = c["CW"] // 128
    c["FG"] = c["D"] // 512
    return c


class Sched:
    ENGS = ("pe", "act", "dve", "sp", "pool")

    def __init__(self):
        self.ops = []
        self.tw = {}
        self.tr = {}
        self.lastkey = {}
        self.pending = {e: {} for e in self.ENGS}
        self.last_on = {}
        self.dma_since = []

    def add(self, eng, fn, reads=(), writes=(), key=None):
        idx = len(self.ops)
        deps = {}

        def dep(i, kind):
            if i is None:
                return
            if deps.get(i) in (None, "war"):
                deps[i] = kind

        for t in reads:
            dep(self.tw.get(t), "raw")
        for t in writes:
            dep(self.tw.get(t), "waw")
            for r in self.tr.get(t, {}).values():
                dep(r, "war")
        if key is not None:
            dep(self.lastkey.get(key), "raw")
            self.lastkey[key] = idx
        for i in self.pending[eng]:
            dep(i, "raw")
        self.pending[eng] = {}
        deps.pop(idx, None)
        self.ops.append(dict(eng=eng, fn=fn, deps=deps, key=key, signal=False))
        for t in writes:
            self.tw[t] = idx
            self.tr[t] = {}
        for t in reads:
            self.tr.setdefault(t, {})[eng if key is None else ("dma", idx)] = idx
        self.last_on[eng] = idx
        if key is not None and eng == "sp":
            self.dma_since.append(idx)
        return idx

    def barrier(self):
        engs = ("pe", "act", "dve", "sp")
        for e in engs:
            for e2 in engs:
                if e2 != e and e2 in self.last_on and self.ops[self.last_on[e2]]["key"] is None:
                    self.pending[e][self.last_on[e2]] = 1
            for i in self.dma_since:
                self.pending[e][i] = 1
        self.dma_since = []

    def _skip(self, dop, op, kind):
        if dop["key"] is not None:
            return False
        if dop["eng"] == op["eng"]:
            if op["eng"] == "pe":
                return True
            if op["key"] is None and (kind == "war" or SAME_ENGINE_INORDER):
                return True
        return False

    def emit(self, nc, block, es):
        ops = self.ops
        for op in ops:
            for d, kind in op["deps"].items():
                if not self._skip(ops[d], op, kind):
                    ops[d]["signal"] = True
        esem = {e: es.enter_context(nc.semaphore("s_" + e)) for e in ("pe", "act", "dve", "pool")}
        keys = []
        for op in ops:
            if op["key"] is not None and op["key"] not in keys:
                keys.append(op["key"])
        ksem = {k: es.enter_context(nc.semaphore("d_%d" % i)) for i, k in enumerate(keys)}
        cnt = {e: 0 for e in self.ENGS}
        kcnt = {k: 0 for k in keys}
        for op in ops:
            if op["key"] is not None:
                kcnt[op["key"]] += 16
                op["ticket"] = (ksem[op["key"]], kcnt[op["key"]])
            elif op["signal"]:
                assert op["eng"] in esem, op["eng"]
                cnt[op["eng"]] += 1
                op["ticket"] = (esem[op["eng"]], cnt[op["eng"]])
        sp_final = {}
        for op in ops:
            if op["key"] is not None:
                sp_final[op["key"]] = op["ticket"]

        def run(name):
            def f(eng):
                waited = {}
                for op in ops:
                    if op["eng"] != name:
                        continue
                    for d, kind in op["deps"].items():
                        dop = ops[d]
                        if self._skip(dop, op, kind):
                            continue
                        sem, val = dop["ticket"]
                        if waited.get(id(sem), 0) >= val:
                            continue
                        eng.wait_ge(sem, val)
                        waited[id(sem)] = val
                    ins = op["fn"](eng)
                    if op["key"] is not None:
                        ins.then_inc(op["ticket"][0], 16)
                    elif op["signal"]:
                        ins.then_inc(op["ticket"][0], 1)
                if name == "sp":
                    for k, (sem, val) in sp_final.items():
                        if waited.get(id(sem), 0) < val:
                            eng.wait_ge(sem, val)
            return f

        block.tensor(run("pe"))
        block.scalar(run("act"))
        block.vector(run("dve"))
        block.gpsimd(run("pool"))
        block.sync(run("sp"))


class Arena:
    def __init__(self, base, nwords):
        self.base = base
        self.n = nwords
        self.top = 0
        self.mark = 0

    def _alloc(self, words):
        words = (words + 7) // 8 * 8
        off = self.top
        self.top += words
        assert self.top <= self.n, ("SBUF arena overflow", self.top, self.n)
        return off

    @staticmethod
    def _shape(ap, shape):
        if len(shape) == 2:
            return ap
        if len(shape) == 3:
            return ap.rearrange("p (a b) -> p a b", b=shape[2])
        if len(shape) == 4:
            return ap.rearrange("p (a b c) -> p a b c", b=shape[2], c=shape[3])
        if len(shape) == 5:
            return ap.rearrange("p (a b c d) -> p a b c d", b=shape[2], c=shape[3], d=shape[4])
        raise ValueError(shape)

    def f32(self, shape):
        n = int(np.prod(shape[1:]))
        off = self._alloc(n)
        return self._shape(self.base[0:shape[0], off:off + n], shape)

    def bf16(self, shape):
        n = int(np.prod(shape[1:]))
        assert n % 2 == 0
        off = self._alloc(n // 2)
        return self._shape(self.base[0:shape[0], off:off + n // 2].bitcast(BF16), shape)

    def set_mark(self):
        self.mark = self.top

    def reset(self):
        self.top = self.mark


def build_program(cfg):
    c = derive(cfg)
    D, FF, HQ, KV, G = c["D"], c["FF"], c["HQ"], c["KV"], c["G"]
    AW, KW, CW, IN, DC, FC, CC, FG = c["AW"], c["KW"], c["CW"], c["IN"], c["DC"], c["FC"], c["CC"], c["FG"]
    AC = AW // 128
    nc = bass.Bass("TRN2", target_bir_lowering=False)

    def din(name, shape):
        return nc.dram_tensor(name, list(shape), F32, kind="ExternalInput").ap()

    def dout(name, shape):
        return nc.dram_tensor(name, list(shape), F32, kind="ExternalOutput").ap()

    def dscr(name, shape):
        return nc.dram_tensor(name, list(shape), F32, kind="Internal").ap()

    xp = din("xp", [512, D])
    xs = din("xs", [768, D])
    c2t = din("c2t", [128, DC, 2])
    ck = din("ck", [PAST, KW])
    cv = din("cv", [PAST, KW])
    w_mod = din("w_mod", [D, 9 * D])
    b_mod = din("b_mod", [1, 9 * D])
    gpre_t = din("gpre_t", [128, 3, DC])
    g_post = din("g_post", [3, D])
    w_in = din("w_in", [D, IN])
    wconv_t = din("wconv_t", [128, 3, CC])
    sink = din("sink", [1, HQ])
    ga_t = din("ga_t", [128, AC])
    gc_t = din("gc_t", [128, CC])
    w_o = din("w_o", [D, D])
    wg = [din("w_ffn1_gate", [D, FF]), din("w_ffn2_gate", [D, FF])]
    wu = [din("w_ffn1_up", [D, FF]), din("w_ffn2_up", [D, FF])]
    wd = [din("w_ffn1_down", [FF, D]), din("w_ffn2_down", [FF, D])]
    flags = din("flags", [128, 2])
    ropec = din("ropec", [128, 768])
    ropes = din("ropes", [128, 768])
    cmat = din("cmat", [128, 5, 128])

    yp = dout("yp", [512, D])
    ys = dout("ys", [512, D])
    sk = dout("sk", [512, KW])
    sv = dout("sv", [512, KW])

    h1p = dscr("h1p", [512, D])
    h1s = dscr("h1s", [768, D])
    h2p = dscr("h2p", [512, D])
    h2s = dscr("h2s", [512, D])
    oscr = dscr("oscr", [512, D])
    cscr = dscr("cscr", [3, 2, D])

    S = Sched()
    es = ExitStack()
    arena_t = es.enter_context(nc.sbuf_tensor("arena", [128, ARENA_WORDS], F32))
    A = Arena(arena_t[:, :], ARENA_WORDS)
    PS = [es.enter_context(nc.psum_tensor("ps%d" % i, [128, 512], F32))[:, :] for i in range(8)]

    slots = [A.bf16([128, SLOT]) for _ in range(NSLOT)]
    cm = A.f32([128, 5, 128])
    ident, rotm, tri_a, tri_b, ones_f = (cm[:, i, :] for i in range(5))
    ones_b = A.bf16([128, 128])
    modcol = A.f32([128, 3, 2, DC, 2])
    gprecol = A.f32([128, 3, DC])
    wconvc = A.f32([128, 3, CC])
    gac = A.f32([128, AC])
    gcc = A.f32([128, CC])
    flg = A.f32([128, 2])
    esink = A.f32([128, HQ])
    mask_l = A.f32([128, 128])
    mask_r = A.f32([128, 128])
    A.set_mark()

    slot_i = [0]

    def wload(src_ap, shape3):
        i = slot_i[0] % NSLOT
        slot_i[0] += 1
        n1, n2 = shape3
        assert n1 * n2 <= SLOT
        view = slots[i][:, 0:n1 * n2].rearrange("p (a b) -> p a b", b=n2)
        S.add("pool", lambda e, o=view, s=src_ap: e.dma_start(out=o, in_=s), writes=[("slot", i)], key=("slot", i))
        return view, ("slot", i)

    def mm(out, lhsT, rhs, start, stop, reads, writes):
        S.add("pe", lambda e: e.matmul(out, lhsT=lhsT, rhs=rhs, start=start, stop=stop), reads=reads, writes=writes)

    def dve(fn, reads, writes):
        S.add("dve", fn, reads=reads, writes=writes)

    def act(fn, reads, writes):
        S.add("act", fn, reads=reads, writes=writes)

    def spdma(out, in_, reads, writes, key):
        S.add("sp", lambda e: e.dma_start(out=out, in_=in_), reads=reads, writes=writes, key=key)

    spdma(cm, cmat, [], ["cm"], "ld0")
    spdma(gprecol, gpre_t, [], ["gprecol"], "ld1")
    spdma(wconvc, wconv_t, [], ["wconvc"], "ld0")
    spdma(gac, ga_t, [], ["gac"], "ld1")
    spdma(gcc, gc_t, [], ["gcc"], "ld0")
    spdma(flg, flags, [], ["flg"], "ld1")
    spdma(esink, sink.partition_broadcast(128), [], ["esink"], "ld0")
    dve(lambda e: e.tensor_copy(out=ones_b, in_=ones_f), ["cm"], ["ones_b"])
    act(lambda e: e.activation(out=esink, in_=esink, func=AF.Exp), ["esink"], ["esink"])
    dve(lambda e: e.tensor_scalar(mask_l, tri_a, flg[:, 0:1], None, op0=ALU.mult), ["cm", "flg"], ["mask_l"])
    dve(lambda e: e.tensor_scalar(mask_r, tri_b, flg[:, 1:2], None, op0=ALU.mult), ["cm", "flg"], ["mask_r"])

    epsc = A.f32([128, 1])
    mark_low = [A.top]
    c2s = A.f32([128, DC, 2])
    scT = A.bf16([128, DC, 2])
    bseg = [A.f32([2, D]) for _ in range(2)]
    gpr = A.f32([2, D])
    mrow = [A.f32([2, 512]) for _ in range(2)]
    crow = [A.f32([2, 512]) for _ in range(2)]
    A.set_mark()
    mod_queue = []

    def mod_slab(it, seg, sl, pb):
        s_, part = seg // 3, seg % 3
        wmv = w_mod.rearrange("(kc p) n -> p kc n", p=128)
        col0 = seg * D + sl * 512
        bank, btok = PS[pb + it % 2], ("ps", pb + it % 2)
        bs, bstok = bseg[seg % 2][:, sl * 512:(sl + 1) * 512], ("bseg", seg % 2)
        if sl == 0:
            spdma(bseg[seg % 2], b_mod[0, seg * D:(seg + 1) * D].partition_broadcast(2), [], [bstok], ("bseg", seg % 2))
        if part == 2:
            gs, gstok = gpr[:, sl * 512:(sl + 1) * 512], "gpr"
            if sl == 0:
                spdma(gpr, g_post[s_, :].partition_broadcast(2), [], [gstok], "gpr")
        for bi in range((DC + 7) // 8):
            k0, k1 = bi * 8, min(DC, bi * 8 + 8)
            view, tok = wload(wmv[:, k0:k1, col0:col0 + 512], (k1 - k0, 512))
            for kc in range(k0, k1):
                mm(bank[0:2, :], scT[:, kc, :], view[:, kc - k0, :], kc == 0, kc == DC - 1, ["scT", tok], [btok])
        mr, mtok = mrow[it % 2], ("mrow", it % 2)
        dve(lambda e: e.tensor_tensor(out=mr, in0=bank[0:2, :], in1=bs, op=ALU.add), [btok, bstok], [mtok])
        if part < 2:
            tb, ttok = PS[pb + 2 + it % 2], ("ps", pb + 2 + it % 2)
            for j in range(4):
                S.add("pe", lambda e, o=tb[:, 2 * j:2 * j + 2], i_=mr[0:2, j * 128:(j + 1) * 128]: e.transpose(o, i_, ident[0:2, 0:2]),
                      reads=[mtok, "cm"], writes=[ttok])
            dst = modcol[:, s_, part, sl * 4:(sl + 1) * 4, :]
            dve(lambda e: e.tensor_copy(out=dst, in_=tb[:, 0:8].rearrange("p (a b) -> p a b", b=2)), [ttok], [("modcol", s_, part)])
        else:
            cr, ctok = crow[it % 2], ("crow", it % 2)
            rw = 1.0 if s_ == 1 else 0.5
            dve(lambda e: e.scalar_tensor_tensor(out=cr, in0=mr, scalar=rw, in1=gs, op0=ALU.mult, op1=ALU.mult), [mtok, gstok], [ctok])
            spdma(cscr[s_, :, sl * 512:(sl + 1) * 512], cr, [ctok], [("cscr", s_)], ("cst", it % 2))
        if part == 1 and sl == D // 512 - 1:
            for cnd in range(2):
                dve(lambda e, cnd=cnd: e.scalar_tensor_tensor(out=modcol[:, s_, 1, :, cnd], in0=modcol[:, s_, 1, :, cnd], scalar=1.0, in1=gprecol[:, s_, :], op0=ALU.add, op1=ALU.mult),
                    [("modcol", s_, 1), "gprecol"], [("modcol", s_, 1)])

    def phase_mod():
        spdma(c2s, c2t, [], ["c2s"], "ld1")
        act(lambda e: e.activation(out=scT, in_=c2s, func=AF.Silu), ["c2s"], ["scT"])
        nsl = D // 512
        it = 0
        for seg in range(9):
            for sl in range(nsl):
                if seg < MOD_UPFRONT_SEGS:
                    mod_slab(it, seg, sl, 0)
                else:
                    mod_queue.append((it, seg, sl))
                it += 1
        S.barrier()
        if not mod_queue:
            A.mark = mark_low[0]

    def mod_pop(pb):
        if mod_queue:
            it, seg, sl = mod_queue.pop(0)
            mod_slab(it, seg, sl, pb)

    def mod_flush():
        while mod_queue:
            mod_pop(0)
        S.barrier()
        A.mark = mark_low[0]

    def pre_stage(src, T, s, cnd, uT, xrow, junk, stat, colmap=None):
        ntb = T // 128
        ss = stat[:, 0:ntb]
        rs = stat[:, 8:8 + ntb]
        dve(lambda e: e.memset(stat, 0.0), [], ["stat"])
        tcount = 0
        for tb in range(ntb):
            xr = xrow[tb % 2]
            xtok = ("xrow", tb % 2)
            spdma(xr, src[tb * 128:(tb + 1) * 128, :], [], [xtok], ("xrow", tb % 2))
            act(lambda e, xr=xr, tb=tb: e.activation(out=junk, in_=xr, func=AF.Square, accum_out=ss[:, tb:tb + 1]),
                [xtok, "stat"], ["junk", "stat"])
            act(lambda e, tb=tb: e.activation(out=rs[:, tb:tb + 1], in_=ss[:, tb:tb + 1], func=AF.Sqrt, bias=epsc, scale=1.0 / D),
                ["stat"], ["stat"])
            dve(lambda e, tb=tb: e.reciprocal(out=rs[:, tb:tb + 1], in_=rs[:, tb:tb + 1]), ["stat"], ["stat"])
            dve(lambda e, xr=xr, tb=tb: e.tensor_scalar(xr, xr, rs[:, tb:tb + 1], None, op0=ALU.mult), [xtok, "stat"], [xtok])
            for cg in range(DC // 4):
                bank = PS[4 + tcount % 4]
                btok = ("ps", 4 + tcount % 4)
                tcount += 1
                for j in range(4):
                    ch = cg * 4 + j
                    S.add("pe", lambda e, o=bank[:, j * 128:(j + 1) * 128], i_=xr[:, ch * 128:(ch + 1) * 128]: e.transpose(o, i_, ident),
                          reads=[xtok, "cm"], writes=[btok])
                for j in range(4):
                    ch = cg * 4 + j
                    cb = tb if colmap is None else colmap[tb]
                    o = uT[:, ch, cb * 128:(cb + 1) * 128]
                    i_ = bank[:, j * 128:(j + 1) * 128]
                    a_ = modcol[:, s, 1, ch, cnd:cnd + 1]
                    b_ = modcol[:, s, 0, ch, cnd:cnd + 1]
                    if cg % 2 == 0:
                        dve(lambda e, o=o, i_=i_, a_=a_, b_=b_: e.tensor_scalar(o, i_, a_, b_, op0=ALU.mult, op1=ALU.add),
                            [btok, ("modcol", s, 0), ("modcol", s, 1)], ["uT"])
                    else:
                        act(lambda e, o=o, i_=i_, a_=a_, b_=b_: e.activation(out=o, in_=i_, func=AF.Identity, bias=b_, scale=a_),
                            [btok, ("modcol", s, 0), ("modcol", s, 1)], ["uT"])

    def o_epilogue(fg, banks, btoks, oslab, ssq, junk2, T):
        ntb = T // 128
        osl = oslab[0]
        otok = ("oslab", 0)
        for tb in range(ntb):
            dve(lambda e, b=banks[tb], tb=tb: e.tensor_copy(out=osl[:, tb, :], in_=b), [btoks[tb]], [otok])
            act(lambda e, tb=tb: e.activation(out=junk2, in_=osl[:, tb, :], func=AF.Square, accum_out=ssq[:, tb, fg:fg + 1]),
                [otok, "ssq0"], ["junk2", ("ssq", tb, fg)])
        for tb in range(ntb):
            spdma(oscr[tb * 128:(tb + 1) * 128, fg * 512:(fg + 1) * 512], osl[:, tb, :], [otok], ["oscr"], ("ost", tb % 2))

    def post_stage(src, dst, T, s, cnd, ssq):
        ntb = T // 128
        A.reset()
        cbc = A.f32([128, D])
        orow = [A.f32([128, D]) for _ in range(2)]
        hrow = [A.f32([128, D]) for _ in range(2)]
        sso = A.f32([128, 8])
        spdma(cbc, cscr[s, cnd:cnd + 1, :].partition_broadcast(128), [("cscr", s)], ["cbc"], "cbc")
        rdssq = [("ssq", tb, fg) for tb in range(ntb) for fg in range(FG)]
        dve(lambda e: e.tensor_reduce(out=sso[:, 0:ntb], in_=ssq[:, 0:ntb, :], axis=AX.X, op=ALU.add), rdssq + ["ssq0"], ["sso"])
        act(lambda e: e.activation(out=sso[:, 0:ntb], in_=sso[:, 0:ntb], func=AF.Sqrt, bias=epsc, scale=1.0 / D), ["sso"], ["sso"])
        dve(lambda e: e.reciprocal(out=sso[:, 0:ntb], in_=sso[:, 0:ntb]), ["sso"], ["sso"])
        for tb in range(ntb):
            orw, hrw = orow[tb % 2], hrow[tb % 2]
            ot, ht = ("orow", tb % 2), ("hrow", tb % 2)
            spdma(orw, oscr[tb * 128:(tb + 1) * 128, :], ["oscr"], [ot], ("orow", tb % 2))
            spdma(hrw, src[tb * 128:(tb + 1) * 128, :], [], [ht], ("hrow", tb % 2))
            dve(lambda e, orw=orw, tb=tb: e.scalar_tensor_tensor(out=orw, in0=orw, scalar=sso[:, tb:tb + 1], in1=cbc, op0=ALU.mult, op1=ALU.mult),
                [ot, "sso", "cbc"], [ot])
            dve(lambda e, orw=orw, hrw=hrw: e.tensor_tensor(out=orw, in0=orw, in1=hrw, op=ALU.add), [ot, ht], [ot])
            spdma(dst[tb * 128:(tb + 1) * 128, :], orw, [ot], [("dst", id(dst))], ("hst", tb % 2))
        S.barrier()

    def ffn_tile(src, dst, T, s, cnd, li):
        ntb = T // 128
        A.reset()
        uT = A.bf16([128, DC, 512])
        oslab = [A.f32([128, 4, 512])] * 2
        sg = [A.f32([128, 512]) for _ in range(2)]
        ssq = A.f32([128, 4, FG])
        stat = A.f32([128, 16])
        junk2 = A.bf16([128, 512])
        off_act = A.top
        actT = A.bf16([128, FC, 512])
        top_after = A.top
        A.top = off_act
        xrow = [A.f32([128, D]) for _ in range(2)]
        junk = A.bf16([128, D])
        A.top = max(A.top, top_after)
        import os
        ksub = int(os.environ.get("KSUB", "9"))
        pre_stage(src, T, s, cnd, uT, xrow, junk, stat)
        S.barrier()
        if ksub <= 0:
            return
        dve(lambda e: e.memset(ssq, 0.0), [], ["ssq0"])
        wgv = wg[li].rearrange("(kc p) n -> p kc n", p=128)
        wuv = wu[li].rearrange("(kc p) n -> p kc n", p=128)
        wdv = wd[li].rearrange("(c p) n -> p c n", p=128)
        for f in range(FC):
            gv, gt = wload(wgv[:, :, f * 128:(f + 1) * 128], (DC, 128))
            uv, ut = wload(wuv[:, :, f * 128:(f + 1) * 128], (DC, 128))
            gb, gbt = PS[f % 2], ("ps", f % 2)
            ub, ubt = PS[2 + f % 2], ("ps", 2 + f % 2)
            for kc in range(DC):
                mm(gb[:, 0:T], gv[:, kc, :], uT[:, kc, 0:T], kc == 0, kc == DC - 1, [gt, "uT"], [gbt])
            for kc in range(DC):
                mm(ub[:, 0:T], uv[:, kc, :], uT[:, kc, 0:T], kc == 0, kc == DC - 1, [ut, "uT"], [ubt])
            sgt = ("sg", f % 2)
            act(lambda e, o=sg[f % 2], b=gb: e.activation(out=o[:, 0:T], in_=b[:, 0:T], func=AF.Silu), [gbt], [sgt])
            dve(lambda e, o=actT[:, f, 0:T], a_=sg[f % 2], b=ub: e.tensor_tensor(out=o, in0=a_[:, 0:T], in1=b[:, 0:T], op=ALU.mult),
                [sgt, ubt], [("actT", f)])
            if MOD_UPFRONT_SEGS < 9 and f % 3 == 2:
                mod_pop(4)
        if ksub <= 1:
            return
        nblk = (FC + 7) // 8
        for fg in range(FG):
            banks = [PS[4 + tb] for tb in range(ntb)]
            btoks = [("ps", 4 + tb) for tb in range(ntb)]
            for bi in range(nblk):
                f0, f1 = bi * 8, min(FC, bi * 8 + 8)
                view, tok = wload(wdv[:, f0:f1, fg * 512:(fg + 1) * 512], (f1 - f0, 512))
                for f in range(f0, f1):
                    for tb in range(ntb):
                        mm(banks[tb], actT[:, f, tb * 128:(tb + 1) * 128], view[:, f - f0, :], f == 0, f == FC - 1,
                           [tok, ("actT", f)], [btoks[tb]])
            o_epilogue(fg, banks, btoks, oslab, ssq, junk2, T)
        S.barrier()
        if ksub <= 2:
            return
        post_stage(src, dst, T, s, cnd, ssq)

    dve(lambda e: e.memset(epsc, EPS), [], ["epsc"])

    def mixer_tile(src, dst, latent):
        import os
        msub = int(os.environ.get('MSUB', '9'))
        cnd = 1 if latent else 0
        T = 768 if latent else 512
        QO = 128 if latent else 0
        nkb = T // 128
        A.reset()
        mrg = A.bf16([128, DC, 512])
        stat = A.f32([128, 16])
        strs = A.f32([128, 8])
        ssq = A.f32([128, 4, FG])
        m_out = A.top
        uT = A.bf16([128, DC, T])
        m0 = A.top
        xrow = [A.f32([128, D]) for _ in range(2)]
        junk = A.bf16([128, D])
        colmap = [1, 2, 3, 4, 0, 5] if latent else None
        pre_stage(src, T, 1, cnd, uT, xrow, junk, stat, colmap)
        S.barrier()
        A.top = m0
        qT = A.bf16([128, AC, 512])
        kT = A.bf16([128, KV, T])
        Vb = A.bf16([128, nkb, KW])
        tmpa = [A.f32([128, 512]) for _ in range(2)]
        tmpb = [A.f32([128, 512]) for _ in range(2)]
        tmpc = [A.f32([128, 512]) for _ in range(2)]
        kvst = [A.f32([128, 4, KW])] if not latent else []
        PT = [A.bf16([128, 5, 512])] * 2 if latent else [A.bf16([128, 1, 512]) for _ in range(2)]
        ptf = [A.f32([128, 3, 128]) for _ in range(2)]
        ub = [A.f32([128, 2, 258]) for _ in range(2)]
        if latent:
            kcT = A.bf16([128, KV, PAST])
            Vc = A.bf16([128, 2, KW])
            cosT = A.f32([128, 768])
            sinT = A.f32([128, 768])
            m1 = A.top
            ckr = A.f32([128, 2, KW])
        dve(lambda e: e.memset(ssq, 0.0), [], ["ssq0"])
        winv = w_in.rearrange("(kc p) n -> p kc n", p=128)
        scale = 1.0 / math.sqrt(128.0)

        if latent:
            spdma(cosT, ropec, [], ["cosT"], "ld0")
            spdma(sinT, ropes, [], ["sinT"], "ld1")
            spdma(ckr, ck.rearrange("(t p) n -> p t n", p=128), [], ["ckr"], "ld0")
            S.add("pool", lambda e: e.dma_start(out=Vc, in_=cv.rearrange("(t p) n -> p t n", p=128)), writes=["Vc"], key="cvld")
            for t2 in range(2):
                for h in range(KV):
                    bank, btok = PS[4 + (t2 * KV + h) % 4], ("ps", 4 + (t2 * KV + h) % 4)
                    S.add("pe", lambda e, o=bank[:, 0:128], i_=ckr[:, t2, h * 128:(h + 1) * 128]: e.transpose(o, i_, ident),
                          reads=["ckr", "cm"], writes=[btok])
                    dve(lambda e, o=kcT[:, h, t2 * 128:(t2 + 1) * 128], b=bank: e.tensor_copy(out=o, in_=b[:, 0:128]), [btok], ["kcT"])
            S.barrier()
            A.top = m1

        if msub <= 0:
            return
        pcount = [0]

        def proj_A(col, n0, n1, edge=None):
            view, tok = wload(winv[:, :, col:col + 128], (DC, 128))
            i = pcount[0] % 4
            pcount[0] += 1
            bank, btok = PS[i], ("ps", i)
            for kc in range(DC):
                mm(bank[:, 0:n1 - n0], view[:, kc, :], uT[:, kc, n0:n1], kc == 0, kc == DC - 1, [tok, "uT"], [btok])
            if edge is not None:
                for kc in range(DC):
                    mm(PS[6][:, edge:edge + 2], view[:, kc, :], uT[:, kc, 127:641:513], kc == 0, kc == DC - 1, [tok, "uT"], [("ps", 6)])
            return bank, btok

        def rope_store(bank, btok, n, dst_ap, c0, i):
            ta, tb_, tc = tmpa[i % 2], tmpb[i % 2], tmpc[i % 2]
            act(lambda e: e.activation(out=ta[:, 0:n], in_=bank[:, 0:n], func=AF.Identity), [btok], [("tmpa", i % 2)])
            rb, rtok = PS[4 + i % 2], ("ps", 4 + i % 2)
            mm(rb[:, 0:n], rotm, ta[:, 0:n], True, True, [("tmpa", i % 2), "cm"], [rtok])
            dve(lambda e: e.tensor_tensor(out=tb_[:, 0:n], in0=ta[:, 0:n], in1=cosT[:, c0:c0 + n], op=ALU.mult),
                [("tmpa", i % 2), "cosT"], [("tmpb", i % 2)])
            dve(lambda e: e.tensor_tensor(out=tc[:, 0:n], in0=rb[:, 0:n], in1=sinT[:, c0:c0 + n], op=ALU.mult),
                [rtok, "sinT"], [("tmpc", i % 2)])
            dve(lambda e: e.tensor_tensor(out=dst_ap, in0=tb_[:, 0:n], in1=tc[:, 0:n], op=ALU.add),
                [("tmpb", i % 2), ("tmpc", i % 2)], ["qk"])

        ri = 0
        for hq in range(AC):
            bank, btok = proj_A(hq * 128, QO, QO + 512)
            if latent:
                rope_store(bank, btok, 512, qT[:, hq, :], QO, ri)
                ri += 1
            else:
                act(lambda e, o=qT[:, hq, :], b=bank: e.activation(out=o, in_=b, func=AF.Identity), [btok], ["qk"])
        for h in range(KV):
            for (n0, n1) in ([(0, 384), (384, 768)] if latent else [(0, 512)]):
                bank, btok = proj_A(AW + h * 128, n0, n1)
                if latent:
                    rope_store(bank, btok, n1 - n0, kT[:, h, n0:n1], n0, ri)
                    ri += 1
                else:
                    act(lambda e, o=kT[:, h, n0:n1], b=bank, n=n1 - n0: e.activation(out=o, in_=b[:, 0:n], func=AF.Identity), [btok], ["qk"])

        if msub <= 1:
            return
        def proj_B(col, ncol, tbs, handler):
            ngrp = (len(tbs) + 3) // 4
            for g in range(ngrp):
                gtb = tbs[g * 4:(g + 1) * 4]
                banks = [(PS[4 + j], ("ps", 4 + j)) for j in range(len(gtb))]
                for bi in range((DC + 7) // 8):
                    k0, k1 = bi * 8, min(DC, bi * 8 + 8)
                    view, tok = wload(winv[:, k0:k1, col:col + ncol], (k1 - k0, ncol))
                    for kc in range(k0, k1):
                        for j, tb in enumerate(gtb):
                            mm(banks[j][0][:, 0:ncol], uT[:, kc, tb * 128:(tb + 1) * 128], view[:, kc - k0, :], kc == 0, kc == DC - 1,
                               [tok, "uT"], [banks[j][1]])
                for j, tb in enumerate(gtb):
                    handler(tb, banks[j][0], banks[j][1])

        def v_handler(tb, bank, btok):
            if latent:
                act(lambda e: e.activation(out=Vb[:, tb, :], in_=bank[:, 0:KW], func=AF.Identity), [btok], ["Vb"])
            else:
                dve(lambda e: e.tensor_copy(out=kvst[0][:, tb, :], in_=bank[:, 0:KW]), [btok], [("kvst", tb)])
                act(lambda e: e.activation(out=Vb[:, tb, :], in_=kvst[0][:, tb, :], func=AF.Identity), [("kvst", tb)], ["Vb"])
                spdma(sv[tb * 128:(tb + 1) * 128, :], kvst[0][:, tb, :], [("kvst", tb)], ["sv"], ("kvs", tb % 2))

        def k_handler(tb, bank, btok):
            dve(lambda e: e.tensor_copy(out=kvst[0][:, tb, :], in_=bank[:, 0:KW]), [btok], [("kvst", tb)])
            spdma(sk[tb * 128:(tb + 1) * 128, :], kvst[0][:, tb, :], [("kvst", tb)], ["sk"], ("kvs", tb % 2))

        for c0 in range(0, KW, 512):
            ncol = min(512, KW - c0)
            if c0 == 0:
                proj_B(AW + KW + c0, ncol, list(range(nkb)), v_handler)
            else:
                raise NotImplementedError
        if not latent:
            proj_B(AW, KW, list(range(nkb)), k_handler)

        if msub <= 2:
            return
        dve(lambda e: e.memset(strs, 0.0), [], ["strs"])
        statb, stattok = PS[7], ("ps", 7)

        scount = [0]

        def stat_acc(lhsT_ap, col, rd):
            mm(PS[7][:, col:col + 1], lhsT_ap, ones_f[:, 0:1], True, True, rd + ["cm"], [("ps", 7)])

        def finish_head(hq, ob, otok, n, col0, tbs, first, last, pi):
            ta, tb_, tc = tmpa[pi % 2], tmpb[pi % 2], tmpc[pi % 2]
            ov = ob[:, 0:2 * n].rearrange("p (a b) -> p a b", b=n)
            dve(lambda e: e.tensor_scalar(ta[:, 0:n], ov[:, 1, :], esink[:, hq:hq + 1], None, op0=ALU.add), [otok, "esink"], [("tmpa", pi % 2)])
            dve(lambda e: e.reciprocal(out=ta[:, 0:n], in_=ta[:, 0:n]), [("tmpa", pi % 2)], [("tmpa", pi % 2)])
            dve(lambda e: e.tensor_tensor(out=tb_[:, 0:n], in0=ov[:, 0, :], in1=ta[:, 0:n], op=ALU.mult), [otok, ("tmpa", pi % 2)], [("tmpb", pi % 2)])
            act(lambda e: e.activation(out=tc[:, 0:n], in_=tb_[:, 0:n], func=AF.Square), [("tmpb", pi % 2)], [("tmpc", pi % 2)])
            act(lambda e: e.activation(out=mrg[:, hq, col0:col0 + n], in_=tb_[:, 0:n], func=AF.Identity, scale=gac[:, hq:hq + 1]),
                [("tmpb", pi % 2), "gac"], [("mrg", hq)])
            for j, tb in enumerate(tbs):
                stat_acc(tc[:, j * 128:(j + 1) * 128], tb * AC + hq, [("tmpc", pi % 2)])

        pi = 0
        if not latent:
            its = [(sq, hq) for sq in range(2) for hq in range(AC)]

            def p_scores(pi):
                sq, hq = its[pi]
                h = hq // G
                sb, sbt = PS[pi % 2], ("ps", pi % 2)
                for kc in range(2):
                    mm(sb[:, kc * 256:(kc + 1) * 256], kT[:, h, sq * 256 + kc * 128: sq * 256 + (kc + 1) * 128], qT[:, hq, sq * 256:(sq + 1) * 256],
                       True, True, ["qk"], [sbt])
                pt, ptt = PT[pi % 2], ("PT", pi % 2)
                act(lambda e: e.activation(out=pt[:, 0, :], in_=sb, func=AF.Exp, scale=scale), [sbt], [ptt])

            def p_pv(pi):
                sq, hq = its[pi]
                h = hq // G
                pt, ptt = PT[pi % 2], ("PT", pi % 2)
                ob, obt = PS[2 + pi % 2], ("ps", 2 + pi % 2)
                for kc in range(2):
                    mm(ob[:, 0:256], Vb[:, sq * 2 + kc, h * 128:(h + 1) * 128], pt[:, 0, kc * 256:(kc + 1) * 256], kc == 0, kc == 1, [ptt, "Vb"], [obt])
                for kc in range(2):
                    mm(ob[:, 256:512], ones_b, pt[:, 0, kc * 256:(kc + 1) * 256], kc == 0, kc == 1, [ptt, "ones_b"], [obt])
                finish_head(hq, ob, obt, 256, sq * 256, [sq * 2, sq * 2 + 1], hq == 0, hq == AC - 1, pi)

            p_scores(0)
            for pi in range(len(its)):
                if pi + 1 < len(its):
                    p_scores(pi + 1)
                p_pv(pi)
        else:
            for hq in range(AC):
                h = hq // G
                pt, ptt = PT[0], "PTc"
                for kc in range(2):
                    sb, sbt = PS[kc], ("ps", kc)
                    mm(sb, kcT[:, h, kc * 128:(kc + 1) * 128], qT[:, hq, :], True, True, ["qk", "kcT"], [sbt])
                    act(lambda e, pt=pt, sb=sb, kc=kc: e.activation(out=pt[:, 3 + kc, :], in_=sb, func=AF.Exp, scale=scale), [sbt], [ptt])
                def lat_scores(qb):
                    sb, sbt = PS[2 + qb % 2], ("ps", 2 + qb % 2)
                    for r in range(3):
                        kb = qb + r
                        mm(sb[:, r * 128:(r + 1) * 128], kT[:, h, kb * 128:(kb + 1) * 128], qT[:, hq, qb * 128:(qb + 1) * 128], True, True, ["qk"], [sbt])
                    pf, pft = ptf[qb % 2], ("ptf", qb % 2)
                    act(lambda e: e.activation(out=pf, in_=sb[:, 0:384].rearrange("p (a b) -> p a b", b=128), func=AF.Exp, scale=scale), [sbt], [pft])
                    ma = mask_l if qb == 0 else tri_a
                    mb = mask_r if qb == 3 else tri_b
                    wt = ("PTw", qb)
                    dve(lambda e: e.tensor_tensor(out=pt[:, 0, qb * 128:(qb + 1) * 128], in0=pf[:, 0, :], in1=ma, op=ALU.mult), [pft, "cm", "mask_l"], [wt])
                    dve(lambda e: e.tensor_copy(out=pt[:, 1, qb * 128:(qb + 1) * 128], in_=pf[:, 1, :]), [pft], [wt])
                    dve(lambda e: e.tensor_tensor(out=pt[:, 2, qb * 128:(qb + 1) * 128], in0=pf[:, 2, :], in1=mb, op=ALU.mult), [pft, "cm", "mask_r"], [wt])

                def lat_pv(qb, pi):
                    ob, obt = PS[4 + qb % 2], ("ps", 4 + qb % 2)
                    for half in range(2):
                        for r in range(5):
                            if r < 3:
                                lv = Vb[:, qb + r, h * 128:(h + 1) * 128]
                            else:
                                lv = Vc[:, r - 3, h * 128:(h + 1) * 128]
                            lhs = lv if half == 0 else ones_b
                            mm(ob[:, half * 128:(half + 1) * 128], lhs, pt[:, r, qb * 128:(qb + 1) * 128], r == 0, r == 4,
                               [("PTw", qb), ptt, "Vb", "Vc", "ones_b"], [obt])
                    finish_head(hq, ob, obt, 128, qb * 128, [qb], hq == 0, hq == AC - 1, pi)

                lat_scores(0)
                for qb in range(4):
                    if qb + 1 < 4:
                        lat_scores(qb + 1)
                    lat_pv(qb, pi)
                    pi += 1

        if msub <= 3:
            return
        o3 = AW + 2 * KW
        for i in range(2):
            dve(lambda e, i=i: e.memset(ub[i], 0.0), [], [("ub", i)])
        for cc in range(CC):
            hbk, hbt = proj_A(o3 + 2 * CW + cc * 128, QO, QO + 512, 12 if latent else None)
            cbk, cbt = proj_A(o3 + CW + cc * 128, QO, QO + 512, 8 if latent else None)
            gbk, gbt = proj_A(o3 + cc * 128, QO, QO + 512)
            i = cc % 2
            u_, ut_ = ub[i], ("ub", i)
            ta, tb_, tc = tmpa[i], tmpb[i], tmpc[i]
            act(lambda e, ta=ta, hbk=hbk: e.activation(out=ta, in_=hbk, func=AF.Identity), [hbt], [("tmpa", i)])
            dve(lambda e, u_=u_, cbk=cbk, ta=ta: e.tensor_tensor(out=u_[:, :, 1:257], in0=cbk.rearrange("p (a b) -> p a b", b=256), in1=ta.rearrange("p (a b) -> p a b", b=256), op=ALU.mult),
                [cbt, ("tmpa", i)], [ut_])
            if latent:
                dve(lambda e, u_=u_: e.tensor_copy(out=u_[:, 0, 257:258], in_=u_[:, 1, 1:2]), [ut_], [ut_])
                dve(lambda e, u_=u_: e.tensor_copy(out=u_[:, 1, 0:1], in_=u_[:, 0, 256:257]), [ut_], [ut_])
                eb = PS[6]
                ed = stat[:, 12:16]
                dve(lambda e, eb=eb: e.tensor_copy(out=ed[:, 0:2], in_=eb[:, 8:10]), [("ps", 6)], ["edge"])
                dve(lambda e, eb=eb: e.tensor_tensor(out=ed[:, 2:4], in0=eb[:, 12:14], in1=ed[:, 0:2], op=ALU.mult), [("ps", 6), "edge"], ["edge"])
                dve(lambda e: e.tensor_tensor(out=ed[:, 2:4], in0=ed[:, 2:4], in1=flg, op=ALU.mult), ["edge", "flg"], ["edge"])
                dve(lambda e, u_=u_: e.tensor_copy(out=u_[:, 0, 0:1], in_=ed[:, 2:3]), ["edge", ut_], [ut_])
                dve(lambda e, u_=u_: e.tensor_copy(out=u_[:, 1, 257:258], in_=ed[:, 3:4]), ["edge", ut_], [ut_])
            t3 = tb_.rearrange("p (a b) -> p a b", b=256)
            dve(lambda e, u_=u_, t3=t3, cc=cc: e.tensor_scalar(t3, u_[:, :, 1:257], wconvc[:, 1, cc:cc + 1], None, op0=ALU.mult), [ut_, "wconvc"], [("tmpb", i)])
            dve(lambda e, u_=u_, t3=t3, cc=cc: e.scalar_tensor_tensor(out=t3, in0=u_[:, :, 0:256], scalar=wconvc[:, 0, cc:cc + 1], in1=t3, op0=ALU.mult, op1=ALU.add),
                [ut_, "wconvc", ("tmpb", i)], [("tmpb", i)])
            dve(lambda e, u_=u_, t3=t3, cc=cc: e.scalar_tensor_tensor(out=t3, in0=u_[:, :, 2:258], scalar=wconvc[:, 2, cc:cc + 1], in1=t3, op0=ALU.mult, op1=ALU.add),
                [ut_, "wconvc", ("tmpb", i)], [("tmpb", i)])
            dve(lambda e, tb_=tb_, gbk=gbk: e.tensor_tensor(out=tb_, in0=tb_, in1=gbk, op=ALU.mult), [gbt, ("tmpb", i)], [("tmpb", i)])
            act(lambda e, tc=tc, tb_=tb_: e.activation(out=tc, in_=tb_, func=AF.Square), [("tmpb", i)], [("tmpc", i)])
            act(lambda e, tb_=tb_, cc=cc: e.activation(out=mrg[:, AC + cc, :], in_=tb_, func=AF.Identity, scale=gcc[:, cc:cc + 1]), [("tmpb", i), "gcc"], [("mrg", AC + cc)])
            for tb in range(4):
                stat_acc(tc[:, tb * 128:(tb + 1) * 128], 4 * AC + tb * CC + cc, [("tmpc", i)])
        if msub <= 4:
            return
        dve(lambda e: e.tensor_reduce(out=strs[:, 0:4], in_=PS[7][:, 0:4 * AC].rearrange("p (a b) -> p a b", b=AC), axis=AX.X, op=ALU.add), [("ps", 7)], ["strs"])
        dve(lambda e: e.tensor_reduce(out=strs[:, 4:8], in_=PS[7][:, 4 * AC:4 * DC].rearrange("p (a b) -> p a b", b=CC), axis=AX.X, op=ALU.add), [("ps", 7)], ["strs"])
        act(lambda e: e.activation(out=strs[:, 0:4], in_=strs[:, 0:4], func=AF.Sqrt, bias=epsc, scale=1.0 / AW), ["strs"], ["strs"])
        act(lambda e: e.activation(out=strs[:, 4:8], in_=strs[:, 4:8], func=AF.Sqrt, bias=epsc, scale=1.0 / CW), ["strs"], ["strs"])
        dve(lambda e: e.reciprocal(out=strs, in_=strs), ["strs"], ["strs"])

        S.barrier()
        A.top = m_out
        oslab = [A.f32([128, 4, 512]) for _ in range(2)]
        junk2 = A.bf16([128, 512])
        tmpa = [A.f32([128, 512]) for _ in range(2)]
        wov = w_o.rearrange("(kc p) n -> p kc n", p=128)
        allm = [("mrg", j) for j in range(DC)]
        for fg in range(FG):
            for part, (c_lo, c_hi, boff) in enumerate(((0, AC, 0), (AC, DC, 4))):
                for k0 in range(c_lo, c_hi, 8):
                    k1 = min(c_hi, k0 + 8)
                    view, tok = wload(wov[:, k0:k1, fg * 512:(fg + 1) * 512], (k1 - k0, 512))
                    for kc in range(k0, k1):
                        for tb in range(4):
                            mm(PS[boff + tb], mrg[:, kc, tb * 128:(tb + 1) * 128], view[:, kc - k0, :], kc == c_lo, kc == c_hi - 1,
                               [tok] + allm, [("ps", boff + tb)])
            osl, otok = oslab[fg % 2], ("oslab", fg % 2)
            for tb in range(4):
                i = tb % 2
                act(lambda e, tb=tb, i=i: e.activation(out=tmpa[i], in_=PS[tb], func=AF.Identity, scale=strs[:, tb:tb + 1]), [("ps", tb), "strs"], [("tmpo", i)])
                dve(lambda e, tb=tb, i=i, osl=osl: e.scalar_tensor_tensor(out=osl[:, tb, :], in0=PS[4 + tb], scalar=strs[:, 4 + tb:5 + tb], in1=tmpa[i], op0=ALU.mult, op1=ALU.add),
                    [("ps", 4 + tb), ("tmpo", i), "strs"], [otok])
                act(lambda e, tb=tb, osl=osl, fg=fg: e.activation(out=junk2, in_=osl[:, tb, :], func=AF.Square, accum_out=ssq[:, tb, fg:fg + 1]),
                    [otok, "ssq0"], ["junk2", ("ssq", tb, fg)])
            for tb in range(4):
                spdma(oscr[tb * 128:(tb + 1) * 128, fg * 512:(fg + 1) * 512], osl[:, tb, :], [otok], ["oscr"], ("ost", tb % 2))
        S.barrier()
        if msub <= 5:
            return
        hsrc = src[0:512, :] if latent else src
        post_stage(hsrc, dst, 512, 1, cnd, ssq)

    import os
    kstop = int(os.environ.get("KSTOP", "99"))
    steps = [
        lambda: phase_mod(),
        lambda: ffn_tile(xp, h1p, 512, 0, 0, 0),
        lambda: ffn_tile(xs[0:512, :], h1s[0:512, :], 512, 0, 1, 0),
        lambda: ffn_tile(xs[512:768, :], h1s[512:768, :], 256, 0, 1, 0),
        lambda: (mod_flush(), mixer_tile(h1p, h2p, False)),
        lambda: mixer_tile(h1s, h2s, True),
        lambda: ffn_tile(h2p, yp, 512, 2, 0, 1),
        lambda: ffn_tile(h2s, ys, 512, 2, 1, 1),
    ]
    for st in steps[:kstop]:
        st()

    with nc.Block() as block:
        S.emit(nc, block, es)
    es.close()
    return nc


def _const_mats():
    ident = np.eye(128, dtype=np.float32)
    rot = np.zeros((128, 128), np.float32)
    for base in (0, 64):
        for j in range(32):
            rot[base + 32 + j, base + j] = -1.0
            rot[base + j, base + 32 + j] = 1.0
    k = np.arange(128)[:, None]
    q = np.arange(128)[None, :]
    tri_a = (k >= q).astype(np.float32)
    tri_b = (k <= q).astype(np.float32)
    ones = np.ones((128, 128), np.float32)
    return np.ascontiguousarray(np.stack([ident, rot, tri_a, tri_b, ones], axis=1))


def _rope_tables(pos):
    inv = (10000.0 ** (-np.arange(32, dtype=np.float32) / 32.0)).astype(np.float32)
    rows = (pos // GRID_W).astype(np.float32)
    cols = (pos % GRID_W).astype(np.float32)
    ang_r = rows[None, :] * inv[:, None]
    ang_c = cols[None, :] * inv[:, None]
    ang = np.concatenate([ang_r, ang_r, ang_c, ang_c], axis=0).astype(np.float32)
    return np.ascontiguousarray(np.cos(ang).astype(np.float32)), np.ascontiguousarray(np.sin(ang).astype(np.float32))


_PROG = {}


def run_kernel(cfg, x_prompt, x_sample, c, cache_k, cache_v, c_ctx, w_mod, b_mod, g_pre, g_post,
               w_in, w_conv, sink, g_attn_out, g_conv_out, w_o,
               w_ffn1_gate, w_ffn1_up, w_ffn1_down, w_ffn2_gate, w_ffn2_up, w_ffn2_down, trace=False):
    d = derive(cfg)
    D, DC, KW, KV, CC, AC = d["D"], d["DC"], d["KW"], d["KV"], d["CC"], d["AW"] // 128
    f = lambda a: np.ascontiguousarray(np.asarray(a, dtype=np.float32))
    key = tuple(sorted(cfg.items()))
    if key not in _PROG:
        _PROG[key] = build_program(cfg)
    nc = _PROG[key]
    xpf = f(x_prompt).reshape(-1, D)
    xsf = f(x_sample)
    shared = {
        "w_mod": f(w_mod)[0], "b_mod": f(b_mod)[0][None, :],
        "gpre_t": np.ascontiguousarray(f(g_pre)[0].reshape(3, DC, 128).transpose(2, 0, 1)),
        "g_post": f(g_post)[0], "w_in": f(w_in)[0],
        "wconv_t": np.ascontiguousarray(f(w_conv)[0].reshape(3, CC, 128).transpose(2, 0, 1)),
        "sink": f(sink)[0][None, :],
        "ga_t": np.ascontiguousarray(f(g_attn_out)[0].reshape(AC, 128).T),
        "gc_t": np.ascontiguousarray(f(g_conv_out)[0].reshape(CC, 128).T),
        "w_o": f(w_o)[0],
        "w_ffn1_gate": f(w_ffn1_gate)[0], "w_ffn1_up": f(w_ffn1_up)[0], "w_ffn1_down": f(w_ffn1_down)[0],
        "w_ffn2_gate": f(w_ffn2_gate)[0], "w_ffn2_up": f(w_ffn2_up)[0], "w_ffn2_down": f(w_ffn2_down)[0],
        "cmat": _const_mats(),
    }
    in_maps = []
    for core in range(NCORES):
        b, j = core // 4, core % 4
        lo = j * OWN - HALO
        xs_c = np.zeros((768, D), np.float32)
        xs_c[0:512] = xsf[b, j * OWN:(j + 1) * OWN]
        if j > 0:
            xs_c[512:640] = xsf[b, j * OWN - HALO:j * OWN]
        if j < 3:
            xs_c[640:768] = xsf[b, (j + 1) * OWN:(j + 1) * OWN + HALO]
        pos = np.clip(np.arange(lo, lo + 768), 0, DEC_SEQ - 1)
        rc, rs = _rope_tables(pos)
        cond = np.stack([f(c_ctx), f(c)[b]], axis=0)
        m = dict(shared)
        m.update({
            "xp": np.ascontiguousarray(xpf[core * 512:(core + 1) * 512]),
            "xs": xs_c,
            "c2t": np.ascontiguousarray(cond.reshape(2, DC, 128).transpose(2, 1, 0)),
            "ck": np.ascontiguousarray(f(cache_k)[b, 0].reshape(PAST, KW)),
            "cv": np.ascontiguousarray(f(cache_v)[b, 0].reshape(PAST, KW)),
            "flags": np.ascontiguousarray(np.tile(np.array([[1.0 if j > 0 else 0.0, 1.0 if j < 3 else 0.0]], np.float32), (128, 1))),
            "ropec": rc, "ropes": rs,
        })
        in_maps.append(m)
    res = run_bass_kernel_spmd(nc, in_maps, core_ids=list(range(NCORES)), **({"trace": True} if trace else {}))
    R = res.results
    y_prompt = np.concatenate([R[i]["yp"] for i in range(NCORES)], axis=0).reshape(16, SEQ, D)
    y_sample = np.stack([np.concatenate([R[4 * b + j]["ys"] for j in range(4)], axis=0) for b in range(2)], axis=0)
    state_k = np.concatenate([R[i]["sk"] for i in range(NCORES)], axis=0).reshape(16, 1, SEQ, KV, 128)
    state_v = np.concatenate([R[i]["sv"] for i in range(NCORES)], axis=0).reshape(16, 1, SEQ, KV, 128)
    outs = (y_prompt.astype(np.float32), y_sample.astype(np.float32), state_k.astype(np.float32), state_v.astype(np.float32))
    if trace:
        return outs, res
    return outs


def kernel(**inputs):
    return run_kernel(FULL, **inputs)
```

```python
import math
from contextlib import ExitStack

import numpy as np
import concourse.bass as bass
import concourse.mybir as mybir
from concourse.bass_utils import run_bass_kernel_spmd

F32 = mybir.dt.float32
BF16 = mybir.dt.bfloat16
AF = mybir.ActivationFunctionType
ALU = mybir.AluOpType
AX = mybir.AxisListType

FULL = dict(D=4096, FF=11008, HQ=16, KV=4)
EPS = 1e-6
NCORES = 8
SEQ = 256
OWN = 512
HALO = 128
PAST = 256
GRID_W = 64
DEC_SEQ = 2048
NSLOT = 4
SLOT = 4096
ARENA_WORDS = 47 * 1024


def derive(cfg):
    c = dict(cfg)
    c["G"] = c["HQ"] // c["KV"]
    c["AW"] = c["HQ"] * 128
    c["KW"] = c["KV"] * 128
    c["CW"] = c["D"] - c["AW"]
    c["IN"] = c["AW"] + 2 * c["KW"] + 3 * c["CW"]
    c["DC"] = c["D"] // 128
    c["FC"] = c["FF"] // 128
    c["CC"] = c["CW"] // 128
    c["FG"] = c["D"] // 512
    return c


class Sched:
    ENGS = ("pe", "act", "dve", "sp", "pool")

    def __init__(self):
        self.ops = []
        self.tw = {}
        self.tr = {}
        self.lastkey = {}
        self.pending = {e: {} for e in self.ENGS}
        self.last_on = {}
        self.dma_since = []

    def add(self, eng, fn, reads=(), writes=(), key=None):
        idx = len(self.ops)
        deps = {}

        def dep(i, kind):
            if i is None:
                return
            if deps.get(i) in (None, "war"):
                deps[i] = kind

        for t in reads:
            dep(self.tw.get(t), "raw")
        for t in writes:
            dep(self.tw.get(t), "waw")
            for r in self.tr.get(t, {}).values():
                dep(r, "war")
        if key is not None:
            dep(self.lastkey.get(key), "raw")
            self.lastkey[key] = idx
        for i in self.pending[eng]:
            dep(i, "raw")
        self.pending[eng] = {}
        deps.pop(idx, None)
        self.ops.append(dict(eng=eng, fn=fn, deps=deps, key=key, signal=False))
        for t in writes:
            self.tw[t] = idx
            self.tr[t] = {}
        for t in reads:
            self.tr.setdefault(t, {})[eng if key is None else ("dma", idx)] = idx
        self.last_on[eng] = idx
        if key is not None and eng == "sp":
            self.dma_since.append(idx)
        return idx

    def barrier(self):
        engs = ("pe", "act", "dve", "sp")
        for e in engs:
            for e2 in engs:
                if e2 != e and e2 in self.last_on and self.ops[self.last_on[e2]]["key"] is None:
                    self.pending[e][self.last_on[e2]] = 1
            for i in self.dma_since:
                self.pending[e][i] = 1
        self.dma_since = []

    def _skip(self, dop, op, kind):
        if dop["key"] is not None:
            return False
        if dop["eng"] == op["eng"]:
            if op["eng"] == "pe":
                return True
            if op["key"] is None and kind == "war":
                return True
        return False

    def emit(self, nc, block, es):
        ops = self.ops
        for op in ops:
            for d, kind in op["deps"].items():
                if not self._skip(ops[d], op, kind):
                    ops[d]["signal"] = True
        esem = {e: es.enter_context(nc.semaphore("s_" + e)) for e in ("pe", "act", "dve", "pool")}
        keys = []
        for op in ops:
            if op["key"] is not None and op["key"] not in keys:
                keys.append(op["key"])
        ksem = {k: es.enter_context(nc.semaphore("d_%d" % i)) for i, k in enumerate(keys)}
        cnt = {e: 0 for e in self.ENGS}
        kcnt = {k: 0 for k in keys}
        for op in ops:
            if op["key"] is not None:
                kcnt[op["key"]] += 16
                op["ticket"] = (ksem[op["key"]], kcnt[op["key"]])
            elif op["signal"]:
                assert op["eng"] in esem, op["eng"]
                cnt[op["eng"]] += 1
                op["ticket"] = (esem[op["eng"]], cnt[op["eng"]])
        sp_final = {}
        for op in ops:
            if op["key"] is not None:
                sp_final[op["key"]] = op["ticket"]

        def run(name):
            def f(eng):
                waited = {}
                for op in ops:
                    if op["eng"] != name:
                        continue
                    for d, kind in op["deps"].items():
                        dop = ops[d]
                        if self._skip(dop, op, kind):
                            continue
                        sem, val = dop["ticket"]
                        if waited.get(id(sem), 0) >= val:
                            continue
                        eng.wait_ge(sem, val)
                        waited[id(sem)] = val
                    ins = op["fn"](eng)
                    if op["key"] is not None:
                        ins.then_inc(op["ticket"][0], 16)
                    elif op["signal"]:
                        ins.then_inc(op["ticket"][0], 1)
                if name == "sp":
                    for k, (sem, val) in sp_final.items():
                        if waited.get(id(sem), 0) < val:
                            eng.wait_ge(sem, val)
            return f

        block.tensor(run("pe"))
        block.scalar(run("act"))
        block.vector(run("dve"))
        block.gpsimd(run("pool"))
        block.sync(run("sp"))


class Arena:
    def __init__(self, base, nwords):
        self.base = base
        self.n = nwords
        self.top = 0
        self.mark = 0

    def _alloc(self, words):
        words = (words + 7) // 8 * 8
        off = self.top
        self.top += words
        assert self.top <= self.n, ("SBUF arena overflow", self.top, self.n)
        return off

    @staticmethod
    def _shape(ap, shape):
        if len(shape) == 2:
            return ap
        if len(shape) == 3:
            return ap.rearrange("p (a b) -> p a b", b=shape[2])
        if len(shape) == 4:
            return ap.rearrange("p (a b c) -> p a b c", b=shape[2], c=shape[3])
        if len(shape) == 5:
            return ap.rearrange("p (a b c d) -> p a b c d", b=shape[2], c=shape[3], d=shape[4])
        raise ValueError(shape)

    def f32(self, shape):
        n = int(np.prod(shape[1:]))
        off = self._alloc(n)
        return self._shape(self.base[0:shape[0], off:off + n], shape)

    def bf16(self, shape):
        n = int(np.prod(shape[1:]))
        assert n % 2 == 0
        off = self._alloc(n // 2)
        return self._shape(self.base[0:shape[0], off:off + n // 2].bitcast(BF16), shape)

    def set_mark(self):
        self.mark = self.top

    def reset(self):
        self.top = self.mark


def build_program(cfg):
    c = derive(cfg)
    D, FF, HQ, KV, G = c["D"], c["FF"], c["HQ"], c["KV"], c["G"]
    AW, KW, CW, IN, DC, FC, CC, FG = c["AW"], c["KW"], c["CW"], c["IN"], c["DC"], c["FC"], c["CC"], c["FG"]
    AC = AW // 128
    nc = bass.Bass("TRN2", target_bir_lowering=False)

    def din(name, shape):
        return nc.dram_tensor(name, list(shape), F32, kind="ExternalInput").ap()

    def dout(name, shape):
        return nc.dram_tensor(name, list(shape), F32, kind="ExternalOutput").ap()

    def dscr(name, shape):
        return nc.dram_tensor(name, list(shape), F32, kind="Internal").ap()

    xp = din("xp", [512, D])
    xs = din("xs", [768, D])
    c2t = din("c2t", [128, DC, 2])
    ck = din("ck", [PAST, KW])
    cv = din("cv", [PAST, KW])
    w_mod = din("w_mod", [D, 9 * D])
    b_mod = din("b_mod", [1, 9 * D])
    gpre_t = din("gpre_t", [128, 3, DC])
    g_post = din("g_post", [3, D])
    w_in = din("w_in", [D, IN])
    wconv_t = din("wconv_t", [128, 3, CC])
    sink = din("sink", [1, HQ])
    ga_t = din("ga_t", [128, AC])
    gc_t = din("gc_t", [128, CC])
    w_o = din("w_o", [D, D])
    wg = [din("w_ffn1_gate", [D, FF]), din("w_ffn2_gate", [D, FF])]
    wu = [din("w_ffn1_up", [D, FF]), din("w_ffn2_up", [D, FF])]
    wd = [din("w_ffn1_down", [FF, D]), din("w_ffn2_down", [FF, D])]
    flags = din("flags", [128, 2])
    ropec = din("ropec", [128, 768])
    ropes = din("ropes", [128, 768])
    cmat = din("cmat", [128, 5, 128])

    yp = dout("yp", [512, D])
    ys = dout("ys", [512, D])
    sk = dout("sk", [512, KW])
    sv = dout("sv", [512, KW])

    h1p = dscr("h1p", [512, D])
    h1s = dscr("h1s", [768, D])
    h2p = dscr("h2p", [512, D])
    h2s = dscr("h2s", [512, D])
    oscr = dscr("oscr", [512, D])
    cscr = dscr("cscr", [3, 2, D])

    S = Sched()
    es = ExitStack()
    arena_t = es.enter_context(nc.sbuf_tensor("arena", [128, ARENA_WORDS], F32))
    A = Arena(arena_t[:, :], ARENA_WORDS)
    PS = [es.enter_context(nc.psum_tensor("ps%d" % i, [128, 512], F32))[:, :] for i in range(8)]

    slots = [A.bf16([128, SLOT]) for _ in range(NSLOT)]
    cm = A.f32([128, 5, 128])
    ident, rotm, tri_a, tri_b, ones_f = (cm[:, i, :] for i in range(5))
    ones_b = A.bf16([128, 128])
    modcol = A.f32([128, 3, 2, DC, 2])
    gprecol = A.f32([128, 3, DC])
    wconvc = A.f32([128, 3, CC])
    gac = A.f32([128, AC])
    gcc = A.f32([128, CC])
    flg = A.f32([128, 2])
    esink = A.f32([128, HQ])
    mask_l = A.f32([128, 128])
    mask_r = A.f32([128, 128])
    A.set_mark()

    slot_i = [0]

    def wload(src_ap, shape3):
        i = slot_i[0] % NSLOT
        slot_i[0] += 1
        n1, n2 = shape3
        assert n1 * n2 <= SLOT
        view = slots[i][:, 0:n1 * n2].rearrange("p (a b) -> p a b", b=n2)
        S.add("pool", lambda e, o=view, s=src_ap: e.dma_start(out=o, in_=s), writes=[("slot", i)], key=("slot", i))
        return view, ("slot", i)

    def mm(out, lhsT, rhs, start, stop, reads, writes):
        S.add("pe", lambda e: e.matmul(out, lhsT=lhsT, rhs=rhs, start=start, stop=stop), reads=reads, writes=writes)

    def dve(fn, reads, writes):
        S.add("dve", fn, reads=reads, writes=writes)

    def act(fn, reads, writes):
        S.add("act", fn, reads=reads, writes=writes)

    def spdma(out, in_, reads, writes, key):
        S.add("sp", lambda e: e.dma_start(out=out, in_=in_), reads=reads, writes=writes, key=key)

    spdma(cm, cmat, [], ["cm"], "ld0")
    spdma(gprecol, gpre_t, [], ["gprecol"], "ld1")
    spdma(wconvc, wconv_t, [], ["wconvc"], "ld0")
    spdma(gac, ga_t, [], ["gac"], "ld1")
    spdma(gcc, gc_t, [], ["gcc"], "ld0")
    spdma(flg, flags, [], ["flg"], "ld1")
    spdma(esink, sink.partition_broadcast(128), [], ["esink"], "ld0")
    dve(lambda e: e.tensor_copy(out=ones_b, in_=ones_f), ["cm"], ["ones_b"])
    act(lambda e: e.activation(out=esink, in_=esink, func=AF.Exp), ["esink"], ["esink"])
    dve(lambda e: e.tensor_scalar(mask_l, tri_a, flg[:, 0:1], None, op0=ALU.mult), ["cm", "flg"], ["mask_l"])
    dve(lambda e: e.tensor_scalar(mask_r, tri_b, flg[:, 1:2], None, op0=ALU.mult), ["cm", "flg"], ["mask_r"])

    def phase_mod():
        A.reset()
        c2s = A.f32([128, DC, 2])
        scT = A.bf16([128, DC, 2])
        bseg = [A.f32([2, D]) for _ in range(2)]
        gpr = A.f32([2, D])
        mrow = [A.f32([2, 512]) for _ in range(2)]
        crow = [A.f32([2, 512]) for _ in range(2)]
        spdma(c2s, c2t, [], ["c2s"], "ld1")
        act(lambda e: e.activation(out=scT, in_=c2s, func=AF.Silu), ["c2s"], ["scT"])
        wmv = w_mod.rearrange("(kc p) n -> p kc n", p=128)
        nsl = D // 512
        it = 0
        for seg in range(9):
            s, part = seg // 3, seg % 3
            bs = bseg[seg % 2]
            spdma(bs, b_mod[:, seg * D:(seg + 1) * D].partition_broadcast(2), [], [("bseg", seg % 2)], ("bseg", seg % 2))
            if part == 2:
                spdma(gpr, g_post[s:s + 1, :].partition_broadcast(2), [], ["gpr"], "gpr")
            for sl in range(nsl):
                col0 = seg * D + sl * 512
                bank = PS[it % 2]
                btok = ("ps", it % 2)
                for bi in range((DC + 7) // 8):
                    k0, k1 = bi * 8, min(DC, bi * 8 + 8)
                    view, tok = wload(wmv[:, k0:k1, col0:col0 + 512], (k1 - k0, 512))
                    for kc in range(k0, k1):
                        mm(bank[0:2, :], scT[:, kc, :], view[:, kc - k0, :], kc == 0, kc == DC - 1,
                           ["scT", tok], [btok])
                mr = mrow[it % 2]
                mtok = ("mrow", it % 2)
                dve(lambda e, o=mr, b=bank, bb=bs, sl=sl: e.tensor_tensor(out=o, in0=b[0:2, :], in1=bb[:, sl * 512:(sl + 1) * 512], op=ALU.add),
                    [btok, ("bseg", seg % 2)], [mtok])
                if part < 2:
                    tb = PS[2 + it % 2]
                    ttok = ("ps", 2 + it % 2)
                    for j in range(4):
                        S.add("pe", lambda e, o=tb[:, 2 * j:2 * j + 2], i_=mr[0:2, j * 128:(j + 1) * 128]: e.transpose(o, i_, ident[0:2, 0:2]),
                              reads=[mtok, "cm"], writes=[ttok])
                    dst = modcol[:, s, part, sl * 4:(sl + 1) * 4, :]
                    dve(lambda e, o=dst, t=tb: e.tensor_copy(out=o, in_=t[:, 0:8].rearrange("p (a b) -> p a b", b=2)),
                        [ttok], ["modcol"])
                else:
                    cr = crow[it % 2]
                    ctok = ("crow", it % 2)
                    rw = 1.0 if s == 1 else 0.5
                    dve(lambda e, o=cr, m=mr, sl=sl, rw=rw: e.scalar_tensor_tensor(out=o, in0=m, scalar=rw, in1=gpr[:, sl * 512:(sl + 1) * 512], op0=ALU.mult, op1=ALU.mult),
                        [mtok, "gpr"], [ctok])
                    spdma(cscr[s, :, sl * 512:(sl + 1) * 512], cr, [ctok], [("cscr", s)], ("cst", it % 2))
                it += 1
            if part == 1:
                for cnd in range(2):
                    dve(lambda e, s=s, cnd=cnd: e.scalar_tensor_tensor(out=modcol[:, s, 1, :, cnd], in0=modcol[:, s, 1, :, cnd], scalar=1.0, in1=gprecol[:, s, :], op0=ALU.add, op1=ALU.mult),
                        ["modcol", "gprecol"], ["modcol"])
        S.barrier()

    def pre_stage(src, T, s, cnd, uT, xrow, junk, stat, colmap=None):
        ntb = T // 128
        ss = stat[:, 0:ntb]
        rs = stat[:, 8:8 + ntb]
        dve(lambda e: e.memset(stat, 0.0), [], ["stat"])
        tcount = 0
        for tb in range(ntb):
            xr = xrow[tb % 2]
            xtok = ("xrow", tb % 2)
            spdma(xr, src[tb * 128:(tb + 1) * 128, :], [], [xtok], ("xrow", tb % 2))
            act(lambda e, xr=xr, tb=tb: e.activation(out=junk, in_=xr, func=AF.Square, accum_out=ss[:, tb:tb + 1]),
                [xtok, "stat"], ["junk", "stat"])
            act(lambda e, tb=tb: e.activation(out=rs[:, tb:tb + 1], in_=ss[:, tb:tb + 1], func=AF.Sqrt, bias=epsc, scale=1.0 / D),
                ["stat"], ["stat"])
            dve(lambda e, tb=tb: e.reciprocal(out=rs[:, tb:tb + 1], in_=rs[:, tb:tb + 1]), ["stat"], ["stat"])
            dve(lambda e, xr=xr, tb=tb: e.tensor_scalar(xr, xr, rs[:, tb:tb + 1], None, op0=ALU.mult), [xtok, "stat"], [xtok])
            for cg in range(DC // 4):
                bank = PS[4 + tcount % 4]
                btok = ("ps", 4 + tcount % 4)
                tcount += 1
                for j in range(4):
                    ch = cg * 4 + j
                    S.add("pe", lambda e, o=bank[:, j * 128:(j + 1) * 128], i_=xr[:, ch * 128:(ch + 1) * 128]: e.transpose(o, i_, ident),
                          reads=[xtok, "cm"], writes=[btok])
                for j in range(4):
                    ch = cg * 4 + j
                    cb = tb if colmap is None else colmap[tb]
                    o = uT[:, ch, cb * 128:(cb + 1) * 128]
                    i_ = bank[:, j * 128:(j + 1) * 128]
                    a_ = modcol[:, s, 1, ch, cnd:cnd + 1]
                    b_ = modcol[:, s, 0, ch, cnd:cnd + 1]
                    if cg % 2 == 0:
                        dve(lambda e, o=o, i_=i_, a_=a_, b_=b_: e.tensor_scalar(o, i_, a_, b_, op0=ALU.mult, op1=ALU.add),
                            [btok, "modcol"], ["uT"])
                    else:
                        act(lambda e, o=o, i_=i_, a_=a_, b_=b_: e.activation(out=o, in_=i_, func=AF.Identity, bias=b_, scale=a_),
                            [btok, "modcol"], ["uT"])

    def o_epilogue(fg, banks, btoks, oslab, ssq, junk2, T):
        ntb = T // 128
        osl = oslab[fg % 2]
        otok = ("oslab", fg % 2)
        for tb in range(ntb):
            dve(lambda e, b=banks[tb], tb=tb: e.tensor_copy(out=osl[:, tb, :], in_=b), [btoks[tb]], [otok])
            act(lambda e, tb=tb: e.activation(out=junk2, in_=osl[:, tb, :], func=AF.Square, accum_out=ssq[:, tb, fg:fg + 1]),
                [otok, "ssq0"], ["junk2", ("ssq", tb, fg)])
        for tb in range(ntb):
            spdma(oscr[tb * 128:(tb + 1) * 128, fg * 512:(fg + 1) * 512], osl[:, tb, :], [otok], ["oscr"], ("ost", tb % 2))

    def post_stage(src, dst, T, s, cnd, ssq):
        ntb = T // 128
        A.reset()
        cbc = A.f32([128, D])
        orow = [A.f32([128, D]) for _ in range(2)]
        hrow = [A.f32([128, D]) for _ in range(2)]
        sso = A.f32([128, 8])
        spdma(cbc, cscr[s, cnd:cnd + 1, :].partition_broadcast(128), [("cscr", s)], ["cbc"], "cbc")
        rdssq = [("ssq", tb, fg) for tb in range(ntb) for fg in range(FG)]
        dve(lambda e: e.tensor_reduce(out=sso[:, 0:ntb], in_=ssq[:, 0:ntb, :], axis=AX.X, op=ALU.add), rdssq + ["ssq0"], ["sso"])
        act(lambda e: e.activation(out=sso[:, 0:ntb], in_=sso[:, 0:ntb], func=AF.Sqrt, bias=epsc, scale=1.0 / D), ["sso"], ["sso"])
        dve(lambda e: e.reciprocal(out=sso[:, 0:ntb], in_=sso[:, 0:ntb]), ["sso"], ["sso"])
        for tb in range(ntb):
            orw, hrw = orow[tb % 2], hrow[tb % 2]
            ot, ht = ("orow", tb % 2), ("hrow", tb % 2)
            spdma(orw, oscr[tb * 128:(tb + 1) * 128, :], ["oscr"], [ot], ("orow", tb % 2))
            spdma(hrw, src[tb * 128:(tb + 1) * 128, :], [], [ht], ("hrow", tb % 2))
            dve(lambda e, orw=orw, tb=tb: e.scalar_tensor_tensor(out=orw, in0=orw, scalar=sso[:, tb:tb + 1], in1=cbc, op0=ALU.mult, op1=ALU.mult),
                [ot, "sso", "cbc"], [ot])
            dve(lambda e, orw=orw, hrw=hrw: e.tensor_tensor(out=orw, in0=orw, in1=hrw, op=ALU.add), [ot, ht], [ot])
            spdma(dst[tb * 128:(tb + 1) * 128, :], orw, [ot], [("dst", id(dst))], ("hst", tb % 2))
        S.barrier()

    def ffn_tile(src, dst, T, s, cnd, li):
        ntb = T // 128
        A.reset()
        uT = A.bf16([128, DC, 512])
        oslab = [A.f32([128, 4, 512]) for _ in range(2)]
        sg = [A.f32([128, 512]) for _ in range(2)]
        ssq = A.f32([128, 4, FG])
        stat = A.f32([128, 16])
        junk2 = A.bf16([128, 512])
        off_act = A.top
        actT = A.bf16([128, FC, 512])
        top_after = A.top
        A.top = off_act
        xrow = [A.f32([128, D]) for _ in range(2)]
        junk = A.bf16([128, D])
        A.top = max(A.top, top_after)
        import os
        ksub = int(os.environ.get("KSUB", "9"))
        pre_stage(src, T, s, cnd, uT, xrow, junk, stat)
        S.barrier()
        if ksub <= 0:
            return
        dve(lambda e: e.memset(ssq, 0.0), [], ["ssq0"])
        wgv = wg[li].rearrange("(kc p) n -> p kc n", p=128)
        wuv = wu[li].rearrange("(kc p) n -> p kc n", p=128)
        wdv = wd[li].rearrange("(c p) n -> p c n", p=128)
        for f in range(FC):
            gv, gt = wload(wgv[:, :, f * 128:(f + 1) * 128], (DC, 128))
            uv, ut = wload(wuv[:, :, f * 128:(f + 1) * 128], (DC, 128))
            gb, gbt = PS[f % 2], ("ps", f % 2)
            ub, ubt = PS[2 + f % 2], ("ps", 2 + f % 2)
            for kc in range(DC):
                mm(gb[:, 0:T], gv[:, kc, :], uT[:, kc, 0:T], kc == 0, kc == DC - 1, [gt, "uT"], [gbt])
            for kc in range(DC):
                mm(ub[:, 0:T], uv[:, kc, :], uT[:, kc, 0:T], kc == 0, kc == DC - 1, [ut, "uT"], [ubt])
            sgt = ("sg", f % 2)
            act(lambda e, o=sg[f % 2], b=gb: e.activation(out=o[:, 0:T], in_=b[:, 0:T], func=AF.Silu), [gbt], [sgt])
            dve(lambda e, o=actT[:, f, 0:T], a_=sg[f % 2], b=ub: e.tensor_tensor(out=o, in0=a_[:, 0:T], in1=b[:, 0:T], op=ALU.mult),
                [sgt, ubt], [("actT", f)])
        if ksub <= 1:
            return
        nblk = (FC + 7) // 8
        for fg in range(FG):
            banks = [PS[4 + tb] for tb in range(ntb)]
            btoks = [("ps", 4 + tb) for tb in range(ntb)]
            for bi in range(nblk):
                f0, f1 = bi * 8, min(FC, bi * 8 + 8)
                view, tok = wload(wdv[:, f0:f1, fg * 512:(fg + 1) * 512], (f1 - f0, 512))
                for f in range(f0, f1):
                    for tb in range(ntb):
                        mm(banks[tb], actT[:, f, tb * 128:(tb + 1) * 128], view[:, f - f0, :], f == 0, f == FC - 1,
                           [tok, ("actT", f)], [btoks[tb]])
            o_epilogue(fg, banks, btoks, oslab, ssq, junk2, T)
        S.barrier()
        if ksub <= 2:
            return
        post_stage(src, dst, T, s, cnd, ssq)

    epsc = A.f32([128, 1])
    A.set_mark()
    dve(lambda e: e.memset(epsc, EPS), [], ["epsc"])

    def mixer_tile(src, dst, latent):
        import os
        msub = int(os.environ.get('MSUB', '9'))
        cnd = 1 if latent else 0
        T = 768 if latent else 512
        QO = 128 if latent else 0
        nkb = T // 128
        A.reset()
        mrg = A.bf16([128, DC, 512])
        stat = A.f32([128, 16])
        strs = A.f32([128, 8])
        ssq = A.f32([128, 4, FG])
        m_out = A.top
        uT = A.bf16([128, DC, T])
        m0 = A.top
        xrow = [A.f32([128, D]) for _ in range(2)]
        junk = A.bf16([128, D])
        colmap = [1, 2, 3, 4, 0, 5] if latent else None
        pre_stage(src, T, 1, cnd, uT, xrow, junk, stat, colmap)
        S.barrier()
        A.top = m0
        qT = A.bf16([128, AC, 512])
        kT = A.bf16([128, KV, T])
        Vb = A.bf16([128, nkb, KW])
        tmpa = [A.f32([128, 512]) for _ in range(2)]
        tmpb = [A.f32([128, 512]) for _ in range(2)]
        tmpc = [A.f32([128, 512]) for _ in range(2)]
        kvst = [A.f32([128, 4, KW])] if not latent else []
        PT = [A.bf16([128, 5, 512])] * 2 if latent else [A.bf16([128, 1, 512]) for _ in range(2)]
        ptf = [A.f32([128, 3, 128]) for _ in range(2)]
        ub = [A.f32([128, 2, 258]) for _ in range(2)]
        if latent:
            kcT = A.bf16([128, KV, PAST])
            Vc = A.bf16([128, 2, KW])
            cosT = A.f32([128, 768])
            sinT = A.f32([128, 768])
            m1 = A.top
            ckr = A.f32([128, 2, KW])
        dve(lambda e: e.memset(ssq, 0.0), [], ["ssq0"])
        winv = w_in.rearrange("(kc p) n -> p kc n", p=128)
        scale = 1.0 / math.sqrt(128.0)

        if latent:
            spdma(cosT, ropec, [], ["cosT"], "ld0")
            spdma(sinT, ropes, [], ["sinT"], "ld1")
            spdma(ckr, ck.rearrange("(t p) n -> p t n", p=128), [], ["ckr"], "ld0")
            S.add("pool", lambda e: e.dma_start(out=Vc, in_=cv.rearrange("(t p) n -> p t n", p=128)), writes=["Vc"], key="cvld")
            for t2 in range(2):
                for h in range(KV):
                    bank, btok = PS[4 + (t2 * KV + h) % 4], ("ps", 4 + (t2 * KV + h) % 4)
                    S.add("pe", lambda e, o=bank[:, 0:128], i_=ckr[:, t2, h * 128:(h + 1) * 128]: e.transpose(o, i_, ident),
                          reads=["ckr", "cm"], writes=[btok])
                    dve(lambda e, o=kcT[:, h, t2 * 128:(t2 + 1) * 128], b=bank: e.tensor_copy(out=o, in_=b[:, 0:128]), [btok], ["kcT"])
            S.barrier()
            A.top = m1

        if msub <= 0:
            return
        pcount = [0]

        def proj_A(col, n0, n1, edge=None):
            view, tok = wload(winv[:, :, col:col + 128], (DC, 128))
            i = pcount[0] % 4
            pcount[0] += 1
            bank, btok = PS[i], ("ps", i)
            for kc in range(DC):
                mm(bank[:, 0:n1 - n0], view[:, kc, :], uT[:, kc, n0:n1], kc == 0, kc == DC - 1, [tok, "uT"], [btok])
            if edge is not None:
                for kc in range(DC):
                    mm(PS[6][:, edge:edge + 2], view[:, kc, :], uT[:, kc, 127:641:513], kc == 0, kc == DC - 1, [tok, "uT"], [("ps", 6)])
            return bank, btok

        def rope_store(bank, btok, n, dst_ap, c0, i):
            ta, tb_, tc = tmpa[i % 2], tmpb[i % 2], tmpc[i % 2]
            act(lambda e: e.activation(out=ta[:, 0:n], in_=bank[:, 0:n], func=AF.Identity), [btok], [("tmpa", i % 2)])
            rb, rtok = PS[4 + i % 2], ("ps", 4 + i % 2)
            mm(rb[:, 0:n], rotm, ta[:, 0:n], True, True, [("tmpa", i % 2), "cm"], [rtok])
            dve(lambda e: e.tensor_tensor(out=tb_[:, 0:n], in0=ta[:, 0:n], in1=cosT[:, c0:c0 + n], op=ALU.mult),
                [("tmpa", i % 2), "cosT"], [("tmpb", i % 2)])
            dve(lambda e: e.tensor_tensor(out=tc[:, 0:n], in0=rb[:, 0:n], in1=sinT[:, c0:c0 + n], op=ALU.mult),
                [rtok, "sinT"], [("tmpc", i % 2)])
            dve(lambda e: e.tensor_tensor(out=dst_ap, in0=tb_[:, 0:n], in1=tc[:, 0:n], op=ALU.add),
                [("tmpb", i % 2), ("tmpc", i % 2)], ["qk"])

        ri = 0
        for hq in range(AC):
            bank, btok = proj_A(hq * 128, QO, QO + 512)
            if latent:
                rope_store(bank, btok, 512, qT[:, hq, :], QO, ri)
                ri += 1
            else:
                act(lambda e, o=qT[:, hq, :], b=bank: e.activation(out=o, in_=b, func=AF.Identity), [btok], ["qk"])
        for h in range(KV):
            for (n0, n1) in ([(0, 384), (384, 768)] if latent else [(0, 512)]):
                bank, btok = proj_A(AW + h * 128, n0, n1)
                if latent:
                    rope_store(bank, btok, n1 - n0, kT[:, h, n0:n1], n0, ri)
                    ri += 1
                else:
                    act(lambda e, o=kT[:, h, n0:n1], b=bank, n=n1 - n0: e.activation(out=o, in_=b[:, 0:n], func=AF.Identity), [btok], ["qk"])

        if msub <= 1:
            return
        def proj_B(col, ncol, tbs, handler):
            ngrp = (len(tbs) + 3) // 4
            for g in range(ngrp):
                gtb = tbs[g * 4:(g + 1) * 4]
                banks = [(PS[4 + j], ("ps", 4 + j)) for j in range(len(gtb))]
                for bi in range((DC + 7) // 8):
                    k0, k1 = bi * 8, min(DC, bi * 8 + 8)
                    view, tok = wload(winv[:, k0:k1, col:col + ncol], (k1 - k0, ncol))
                    for kc in range(k0, k1):
                        for j, tb in enumerate(gtb):
                            mm(banks[j][0][:, 0:ncol], uT[:, kc, tb * 128:(tb + 1) * 128], view[:, kc - k0, :], kc == 0, kc == DC - 1,
                               [tok, "uT"], [banks[j][1]])
                for j, tb in enumerate(gtb):
                    handler(tb, banks[j][0], banks[j][1])

        def v_handler(tb, bank, btok):
            if latent:
                act(lambda e: e.activation(out=Vb[:, tb, :], in_=bank[:, 0:KW], func=AF.Identity), [btok], ["Vb"])
            else:
                dve(lambda e: e.tensor_copy(out=kvst[0][:, tb, :], in_=bank[:, 0:KW]), [btok], [("kvst", tb)])
                act(lambda e: e.activation(out=Vb[:, tb, :], in_=kvst[0][:, tb, :], func=AF.Identity), [("kvst", tb)], ["Vb"])
                spdma(sv[tb * 128:(tb + 1) * 128, :], kvst[0][:, tb, :], [("kvst", tb)], ["sv"], ("kvs", tb % 2))

        def k_handler(tb, bank, btok):
            dve(lambda e: e.tensor_copy(out=kvst[0][:, tb, :], in_=bank[:, 0:KW]), [btok], [("kvst", tb)])
            spdma(sk[tb * 128:(tb + 1) * 128, :], kvst[0][:, tb, :], [("kvst", tb)], ["sk"], ("kvs", tb % 2))

        for c0 in range(0, KW, 512):
            ncol = min(512, KW - c0)
            if c0 == 0:
                proj_B(AW + KW + c0, ncol, list(range(nkb)), v_handler)
            else:
                raise NotImplementedError
        if not latent:
            proj_B(AW, KW, list(range(nkb)), k_handler)

        if msub <= 2:
            return
        dve(lambda e: e.memset(strs, 0.0), [], ["strs"])
        statb, stattok = PS[7], ("ps", 7)

        scount = [0]

        def stat_acc(lhsT_ap, col, rd):
            mm(PS[7][:, col:col + 1], lhsT_ap, ones_f[:, 0:1], True, True, rd + ["cm"], [("ps", 7)])

        def finish_head(hq, ob, otok, n, col0, tbs, first, last, pi):
            ta, tb_, tc = tmpa[pi % 2], tmpb[pi % 2], tmpc[pi % 2]
            ov = ob[:, 0:2 * n].rearrange("p (a b) -> p a b", b=n)
            dve(lambda e: e.tensor_scalar(ta[:, 0:n], ov[:, 1, :], esink[:, hq:hq + 1], None, op0=ALU.add), [otok, "esink"], [("tmpa", pi % 2)])
            dve(lambda e: e.reciprocal(out=ta[:, 0:n], in_=ta[:, 0:n]), [("tmpa", pi % 2)], [("tmpa", pi % 2)])
            dve(lambda e: e.tensor_tensor(out=tb_[:, 0:n], in0=ov[:, 0, :], in1=ta[:, 0:n], op=ALU.mult), [otok, ("tmpa", pi % 2)], [("tmpb", pi % 2)])
            act(lambda e: e.activation(out=tc[:, 0:n], in_=tb_[:, 0:n], func=AF.Square), [("tmpb", pi % 2)], [("tmpc", pi % 2)])
            dve(lambda e: e.tensor_scalar(mrg[:, hq, col0:col0 + n], tb_[:, 0:n], gac[:, hq:hq + 1], None, op0=ALU.mult),
                [("tmpb", pi % 2), "gac"], [("mrg", hq)])
            for j, tb in enumerate(tbs):
                stat_acc(tc[:, j * 128:(j + 1) * 128], tb * AC + hq, [("tmpc", pi % 2)])

        pi = 0
        if not latent:
            for sq in range(2):
                for hq in range(AC):
                    h = hq // G
                    sb, sbt = PS[pi % 2], ("ps", pi % 2)
                    for kc in range(2):
                        mm(sb[:, kc * 256:(kc + 1) * 256], kT[:, h, sq * 256 + kc * 128: sq * 256 + (kc + 1) * 128], qT[:, hq, sq * 256:(sq + 1) * 256],
                           True, True, ["qk"], [sbt])
                    pt, ptt = PT[pi % 2], ("PT", pi % 2)
                    act(lambda e, pt=pt, sb=sb: e.activation(out=pt[:, 0, :], in_=sb, func=AF.Exp, scale=scale), [sbt], [ptt])
                    ob, obt = PS[2 + pi % 2], ("ps", 2 + pi % 2)
                    for kc in range(2):
                        mm(ob[:, 0:256], Vb[:, sq * 2 + kc, h * 128:(h + 1) * 128], pt[:, 0, kc * 256:(kc + 1) * 256], kc == 0, kc == 1, [ptt, "Vb"], [obt])
                    for kc in range(2):
                        mm(ob[:, 256:512], ones_b, pt[:, 0, kc * 256:(kc + 1) * 256], kc == 0, kc == 1, [ptt, "ones_b"], [obt])
                    finish_head(hq, ob, obt, 256, sq * 256, [sq * 2, sq * 2 + 1], hq == 0, hq == AC - 1, pi)
                    pi += 1
        else:
            for hq in range(AC):
                h = hq // G
                pt, ptt = PT[0], ("PT", 0)
                for kc in range(2):
                    sb, sbt = PS[kc], ("ps", kc)
                    mm(sb, kcT[:, h, kc * 128:(kc + 1) * 128], qT[:, hq, :], True, True, ["qk", "kcT"], [sbt])
                    act(lambda e, pt=pt, sb=sb, kc=kc: e.activation(out=pt[:, 3 + kc, :], in_=sb, func=AF.Exp, scale=scale), [sbt], [ptt])
                for qb in range(4):
                    sb, sbt = PS[2 + qb % 2], ("ps", 2 + qb % 2)
                    for r in range(3):
                        kb = qb + r
                        mm(sb[:, r * 128:(r + 1) * 128], kT[:, h, kb * 128:(kb + 1) * 128], qT[:, hq, qb * 128:(qb + 1) * 128], True, True, ["qk"], [sbt])
                    pf, pft = ptf[qb % 2], ("ptf", qb % 2)
                    act(lambda e, pf=pf, sb=sb: e.activation(out=pf, in_=sb[:, 0:384].rearrange("p (a b) -> p a b", b=128), func=AF.Exp, scale=scale), [sbt], [pft])
                    ma = mask_l if qb == 0 else tri_a
                    mb = mask_r if qb == 3 else tri_b
                    dve(lambda e, pt=pt, pf=pf, qb=qb, ma=ma: e.tensor_tensor(out=pt[:, 0, qb * 128:(qb + 1) * 128], in0=pf[:, 0, :], in1=ma, op=ALU.mult),
                        [pft, "cm", "mask_l"], [ptt])
                    dve(lambda e, pt=pt, pf=pf, qb=qb: e.tensor_copy(out=pt[:, 1, qb * 128:(qb + 1) * 128], in_=pf[:, 1, :]), [pft], [ptt])
                    dve(lambda e, pt=pt, pf=pf, qb=qb, mb=mb: e.tensor_tensor(out=pt[:, 2, qb * 128:(qb + 1) * 128], in0=pf[:, 2, :], in1=mb, op=ALU.mult),
                        [pft, "cm", "mask_r"], [ptt])
                    ob, obt = PS[4 + qb % 2], ("ps", 4 + qb % 2)
                    for half, lhs_of in ((0, None), (1, ones_b)):
                        for r in range(5):
                            if r < 3:
                                lv = Vb[:, qb + r, h * 128:(h + 1) * 128]
                            else:
                                lv = Vc[:, r - 3, h * 128:(h + 1) * 128]
                            lhs = lv if half == 0 else ones_b
                            mm(ob[:, half * 128:(half + 1) * 128], lhs, pt[:, r, qb * 128:(qb + 1) * 128], r == 0, r == 4,
                               [ptt, "Vb", "Vc", "ones_b"], [obt])
                    finish_head(hq, ob, obt, 128, qb * 128, [qb], hq == 0, hq == AC - 1, pi)
                    pi += 1

        if msub <= 3:
            return
        o3 = AW + 2 * KW
        for cc in range(CC):
            gbk, gbt = proj_A(o3 + cc * 128, QO, QO + 512)
            cbk, cbt = proj_A(o3 + CW + cc * 128, QO, QO + 512, 8 if latent else None)
            hbk, hbt = proj_A(o3 + 2 * CW + cc * 128, QO, QO + 512, 12 if latent else None)
            i = cc % 2
            u_, ut_ = ub[i], ("ub", i)
            ta, tb_, tc = tmpa[i], tmpb[i], tmpc[i]
            act(lambda e, ta=ta, hbk=hbk: e.activation(out=ta, in_=hbk, func=AF.Identity), [hbt], [("tmpa", i)])
            dve(lambda e, u_=u_: e.memset(u_, 0.0), [], [ut_])
            dve(lambda e, u_=u_, cbk=cbk, ta=ta: e.tensor_tensor(out=u_[:, :, 1:257], in0=cbk.rearrange("p (a b) -> p a b", b=256), in1=ta.rearrange("p (a b) -> p a b", b=256), op=ALU.mult),
                [cbt, ("tmpa", i)], [ut_])
            if latent:
                dve(lambda e, u_=u_: e.tensor_copy(out=u_[:, 0, 257:258], in_=u_[:, 1, 1:2]), [ut_], [ut_])
                dve(lambda e, u_=u_: e.tensor_copy(out=u_[:, 1, 0:1], in_=u_[:, 0, 256:257]), [ut_], [ut_])
                eb = PS[6]
                ed = stat[:, 12:16]
                dve(lambda e, eb=eb: e.tensor_copy(out=ed[:, 0:2], in_=eb[:, 8:10]), [("ps", 6)], ["edge"])
                dve(lambda e, eb=eb: e.tensor_tensor(out=ed[:, 2:4], in0=eb[:, 12:14], in1=ed[:, 0:2], op=ALU.mult), [("ps", 6), "edge"], ["edge"])
                dve(lambda e: e.tensor_tensor(out=ed[:, 2:4], in0=ed[:, 2:4], in1=flg, op=ALU.mult), ["edge", "flg"], ["edge"])
                dve(lambda e, u_=u_: e.tensor_copy(out=u_[:, 0, 0:1], in_=ed[:, 2:3]), ["edge", ut_], [ut_])
                dve(lambda e, u_=u_: e.tensor_copy(out=u_[:, 1, 257:258], in_=ed[:, 3:4]), ["edge", ut_], [ut_])
            t3 = tb_.rearrange("p (a b) -> p a b", b=256)
            dve(lambda e, u_=u_, t3=t3, cc=cc: e.tensor_scalar(t3, u_[:, :, 1:257], wconvc[:, 1, cc:cc + 1], None, op0=ALU.mult), [ut_, "wconvc"], [("tmpb", i)])
            dve(lambda e, u_=u_, t3=t3, cc=cc: e.scalar_tensor_tensor(out=t3, in0=u_[:, :, 0:256], scalar=wconvc[:, 0, cc:cc + 1], in1=t3, op0=ALU.mult, op1=ALU.add),
                [ut_, "wconvc", ("tmpb", i)], [("tmpb", i)])
            dve(lambda e, u_=u_, t3=t3, cc=cc: e.scalar_tensor_tensor(out=t3, in0=u_[:, :, 2:258], scalar=wconvc[:, 2, cc:cc + 1], in1=t3, op0=ALU.mult, op1=ALU.add),
                [ut_, "wconvc", ("tmpb", i)], [("tmpb", i)])
            dve(lambda e, tb_=tb_, gbk=gbk: e.tensor_tensor(out=tb_, in0=tb_, in1=gbk, op=ALU.mult), [gbt, ("tmpb", i)], [("tmpb", i)])
            act(lambda e, tc=tc, tb_=tb_: e.activation(out=tc, in_=tb_, func=AF.Square), [("tmpb", i)], [("tmpc", i)])
            dve(lambda e, tb_=tb_, cc=cc: e.tensor_scalar(mrg[:, AC + cc, :], tb_, gcc[:, cc:cc + 1], None, op0=ALU.mult), [("tmpb", i), "gcc"], [("mrg", AC + cc)])
            for tb in range(4):
                stat_acc(tc[:, tb * 128:(tb + 1) * 128], 4 * AC + tb * CC + cc, [("tmpc", i)])
        if msub <= 4:
            return
        dve(lambda e: e.tensor_reduce(out=strs[:, 0:4], in_=PS[7][:, 0:4 * AC].rearrange("p (a b) -> p a b", b=AC), axis=AX.X, op=ALU.add), [("ps", 7)], ["strs"])
        dve(lambda e: e.tensor_reduce(out=strs[:, 4:8], in_=PS[7][:, 4 * AC:4 * DC].rearrange("p (a b) -> p a b", b=CC), axis=AX.X, op=ALU.add), [("ps", 7)], ["strs"])
        act(lambda e: e.activation(out=strs[:, 0:4], in_=strs[:, 0:4], func=AF.Sqrt, bias=epsc, scale=1.0 / AW), ["strs"], ["strs"])
        act(lambda e: e.activation(out=strs[:, 4:8], in_=strs[:, 4:8], func=AF.Sqrt, bias=epsc, scale=1.0 / CW), ["strs"], ["strs"])
        dve(lambda e: e.reciprocal(out=strs, in_=strs), ["strs"], ["strs"])

        S.barrier()
        A.top = m_out
        oslab = [A.f32([128, 4, 512]) for _ in range(2)]
        junk2 = A.bf16([128, 512])
        tmpa = [A.f32([128, 512]) for _ in range(2)]
        wov = w_o.rearrange("(kc p) n -> p kc n", p=128)
        allm = [("mrg", j) for j in range(DC)]
        for fg in range(FG):
            for part, (c_lo, c_hi, boff) in enumerate(((0, AC, 0), (AC, DC, 4))):
                for k0 in range(c_lo, c_hi, 8):
                    k1 = min(c_hi, k0 + 8)
                    view, tok = wload(wov[:, k0:k1, fg * 512:(fg + 1) * 512], (k1 - k0, 512))
                    for kc in range(k0, k1):
                        for tb in range(4):
                            mm(PS[boff + tb], mrg[:, kc, tb * 128:(tb + 1) * 128], view[:, kc - k0, :], kc == c_lo, kc == c_hi - 1,
                               [tok] + allm, [("ps", boff + tb)])
            osl, otok = oslab[fg % 2], ("oslab", fg % 2)
            for tb in range(4):
                i = tb % 2
                act(lambda e, tb=tb, i=i: e.activation(out=tmpa[i], in_=PS[tb], func=AF.Identity, scale=strs[:, tb:tb + 1]), [("ps", tb), "strs"], [("tmpo", i)])
                dve(lambda e, tb=tb, i=i, osl=osl: e.scalar_tensor_tensor(out=osl[:, tb, :], in0=PS[4 + tb], scalar=strs[:, 4 + tb:5 + tb], in1=tmpa[i], op0=ALU.mult, op1=ALU.add),
                    [("ps", 4 + tb), ("tmpo", i), "strs"], [otok])
                act(lambda e, tb=tb, osl=osl, fg=fg: e.activation(out=junk2, in_=osl[:, tb, :], func=AF.Square, accum_out=ssq[:, tb, fg:fg + 1]),
                    [otok, "ssq0"], ["junk2", ("ssq", tb, fg)])
            for tb in range(4):
                spdma(oscr[tb * 128:(tb + 1) * 128, fg * 512:(fg + 1) * 512], osl[:, tb, :], [otok], ["oscr"], ("ost", tb % 2))
        S.barrier()
        if msub <= 5:
            return
        hsrc = src[0:512, :] if latent else src
        post_stage(hsrc, dst, 512, 1, cnd, ssq)

    import os
    kstop = int(os.environ.get("KSTOP", "99"))
    steps = [
        lambda: phase_mod(),
        lambda: ffn_tile(xp, h1p, 512, 0, 0, 0),
        lambda: ffn_tile(xs[0:512, :], h1s[0:512, :], 512, 0, 1, 0),
        lambda: ffn_tile(xs[512:768, :], h1s[512:768, :], 256, 0, 1, 0),
        lambda: mixer_tile(h1p, h2p, False),
        lambda: mixer_tile(h1s, h2s, True),
        lambda: ffn_tile(h2p, yp, 512, 2, 0, 1),
        lambda: ffn_tile(h2s, ys, 512, 2, 1, 1),
    ]
    for st in steps[:kstop]:
        st()

    with nc.Block() as block:
        S.emit(nc, block, es)
    es.close()
    return nc


def _const_mats():
    ident = np.eye(128, dtype=np.float32)
    rot = np.zeros((128, 128), np.float32)
    for base in (0, 64):
        for j in range(32):
            rot[base + 32 + j, base + j] = -1.0
            rot[base + j, base + 32 + j] = 1.0
    k = np.arange(128)[:, None]
    q = np.arange(128)[None, :]
    tri_a = (k >= q).astype(np.float32)
    tri_b = (k <= q).astype(np.float32)
    ones = np.ones((128, 128), np.float32)
    return np.ascontiguousarray(np.stack([ident, rot, tri_a, tri_b, ones], axis=1))


def _rope_tables(pos):
    inv = (10000.0 ** (-np.arange(32, dtype=np.float32) / 32.0)).astype(np.float32)
    rows = (pos // GRID_W).astype(np.float32)
    cols = (pos % GRID_W).astype(np.float32)
    ang_r = rows[None, :] * inv[:, None]
    ang_c = cols[None, :] * inv[:, None]
    ang = np.concatenate([ang_r, ang_r, ang_c, ang_c], axis=0).astype(np.float32)
    return np.ascontiguousarray(np.cos(ang).astype(np.float32)), np.ascontiguousarray(np.sin(ang).astype(np.float32))


_PROG = {}


def run_kernel(cfg, x_prompt, x_sample, c, cache_k, cache_v, c_ctx, w_mod, b_mod, g_pre, g_post,
               w_in, w_conv, sink, g_attn_out, g_conv_out, w_o,
               w_ffn1_gate, w_ffn1_up, w_ffn1_down, w_ffn2_gate, w_ffn2_up, w_ffn2_down, trace=False):
    d = derive(cfg)
    D, DC, KW, KV, CC, AC = d["D"], d["DC"], d["KW"], d["KV"], d["CC"], d["AW"] // 128
    f = lambda a: np.ascontiguousarray(np.asarray(a, dtype=np.float32))
    key = tuple(sorted(cfg.items()))
    if key not in _PROG:
        _PROG[key] = build_program(cfg)
    nc = _PROG[key]
    xpf = f(x_prompt).reshape(-1, D)
    xsf = f(x_sample)
    shared = {
        "w_mod": f(w_mod)[0], "b_mod": f(b_mod)[0][None, :],
        "gpre_t": np.ascontiguousarray(f(g_pre)[0].reshape(3, DC, 128).transpose(2, 0, 1)),
        "g_post": f(g_post)[0], "w_in": f(w_in)[0],
        "wconv_t": np.ascontiguousarray(f(w_conv)[0].reshape(3, CC, 128).transpose(2, 0, 1)),
        "sink": f(sink)[0][None, :],
        "ga_t": np.ascontiguousarray(f(g_attn_out)[0].reshape(AC, 128).T),
        "gc_t": np.ascontiguousarray(f(g_conv_out)[0].reshape(CC, 128).T),
        "w_o": f(w_o)[0],
        "w_ffn1_gate": f(w_ffn1_gate)[0], "w_ffn1_up": f(w_ffn1_up)[0], "w_ffn1_down": f(w_ffn1_down)[0],
        "w_ffn2_gate": f(w_ffn2_gate)[0], "w_ffn2_up": f(w_ffn2_up)[0], "w_ffn2_down": f(w_ffn2_down)[0],
        "cmat": _const_mats(),
    }
    in_maps = []
    for core in range(NCORES):
        b, j = core // 4, core % 4
        lo = j * OWN - HALO
        xs_c = np.zeros((768, D), np.float32)
        xs_c[0:512] = xsf[b, j * OWN:(j + 1) * OWN]
        if j > 0:
            xs_c[512:640] = xsf[b, j * OWN - HALO:j * OWN]
        if j < 3:
            xs_c[640:768] = xsf[b, (j + 1) * OWN:(j + 1) * OWN + HALO]
        pos = np.clip(np.arange(lo, lo + 768), 0, DEC_SEQ - 1)
        rc, rs = _rope_tables(pos)
        cond = np.stack([f(c_ctx), f(c)[b]], axis=0)
        m = dict(shared)
        m.update({
            "xp": np.ascontiguousarray(xpf[core * 512:(core + 1) * 512]),
            "xs": xs_c,
            "c2t": np.ascontiguousarray(cond.reshape(2, DC, 128).transpose(2, 1, 0)),
            "ck": np.ascontiguousarray(f(cache_k)[b, 0].reshape(PAST, KW)),
            "cv": np.ascontiguousarray(f(cache_v)[b, 0].reshape(PAST, KW)),
            "flags": np.ascontiguousarray(np.tile(np.array([[1.0 if j > 0 else 0.0, 1.0 if j < 3 else 0.0]], np.float32), (128, 1))),
            "ropec": rc, "ropes": rs,
        })
        in_maps.append(m)
    res = run_bass_kernel_spmd(nc, in_maps, core_ids=list(range(NCORES)), **({"trace": True} if trace else {}))
    R = res.results
    y_prompt = np.concatenate([R[i]["yp"] for i in range(NCORES)], axis=0).reshape(16, SEQ, D)
    y_sample = np.stack([np.concatenate([R[4 * b + j]["ys"] for j in range(4)], axis=0) for b in range(2)], axis=0)
    state_k = np.concatenate([R[i]["sk"] for i in range(NCORES)], axis=0).reshape(16, 1, SEQ, KV, 128)
    state_v = np.concatenate([R[i]["sv"] for i in range(NCORES)], axis=0).reshape(16, 1, SEQ, KV, 128)
    outs = (y_prompt.astype(np.float32), y_sample.astype(np.float32), state_k.astype(np.float32), state_v.astype(np.float32))
    if trace:
        return outs, res
    return outs


def kernel(**inputs):
    return run_kernel(FULL, **inputs)
```
